# Optimizing a Trainium2 kernel written in Bass

```python
import math
import jax, jax.numpy as jnp
from jax import lax
import numpy as np

D_MODEL = 1024
BATCH = 16
SEQ = 2048
DEPTH = 2

S5_WIDTH = D_MODEL // 4
S5_GROUP = 16
S5_GROUPS = S5_WIDTH // S5_GROUP
S5_STATE = 64
GLA_HEADS = 4
GLA_WIDTH = D_MODEL // 4
GLA_DV = GLA_WIDTH // GLA_HEADS
GLA_DK = GLA_DV // 2
GLA_QK = GLA_HEADS * GLA_DK
GLA_RANK = 16
GLA_TAU = 16.0
GLA_CHUNK = 64
SWA_WIDTH = D_MODEL - S5_WIDTH - GLA_WIDTH
SWA_HEAD_DIM = 64
SWA_HEADS = SWA_WIDTH // SWA_HEAD_DIM
SWA_KV_HEADS = 2
SWA_KV = SWA_KV_HEADS * SWA_HEAD_DIM
SWA_WINDOW = 128
SWA_BLOCK = 128
ROT_DIM = SWA_HEAD_DIM // 4
ROPE_THETA = 500000.0
D_FF = 4 * D_MODEL
LN_EPS = 1e-5
DEEPNORM_ALPHA = (2 * DEPTH) ** 0.25
DEEPNORM_BETA = (8 * DEPTH) ** -0.25
NEG_BIG = -1e30

IN_SIZES = (S5_WIDTH, GLA_QK, GLA_QK, GLA_WIDTH, GLA_WIDTH, GLA_RANK, GLA_RANK,
            SWA_WIDTH, SWA_KV, SWA_KV)
D_IN = sum(IN_SIZES)

kernel_name = "hybrid_s5_gla_swa_deepnorm_encoder"


def layer_norm(x, g, b):
    xf = x.astype(jnp.float32)
    mu = jnp.mean(xf, axis=-1, keepdims=True)
    var = jnp.mean(jnp.square(xf - mu), axis=-1, keepdims=True)
    y = (xf - mu) * lax.rsqrt(var + LN_EPS) * g.astype(jnp.float32) + b.astype(jnp.float32)
    return y.astype(x.dtype)


def _linear_rec(lhs, rhs):
    a1, b1 = lhs
    a2, b2 = rhs
    return a1 * a2, a2 * b1 + b2


def s5_mixer(u, a_re, a_im, log_step, b_re, b_im, c_re, c_im, d_skip, w_glu, b_glu):
    f32 = jnp.float32
    bsz, L, _ = u.shape
    u = u.astype(f32).reshape(bsz, L, S5_GROUPS, S5_GROUP)
    lam = lax.complex(a_re.astype(f32), a_im.astype(f32))
    step = jnp.exp(log_step.astype(f32))
    lam_bar = jnp.exp(lam * step)
    b_c = lax.complex(b_re.astype(f32), b_im.astype(f32))
    b_bar = ((lam_bar - 1.0) / lam)[..., None] * b_c
    bu = jnp.einsum('zgph,blgh->zblgp', b_bar, u.astype(jnp.complex64))
    a_f = jnp.broadcast_to(lam_bar[0], (1, L, S5_GROUPS, S5_STATE))
    a_b = jnp.broadcast_to(lam_bar[1], (1, L, S5_GROUPS, S5_STATE))
    _, h_f = lax.associative_scan(_linear_rec, (a_f, bu[0]), axis=1)
    _, h_b = lax.associative_scan(_linear_rec, (a_b, bu[1]), axis=1, reverse=True)
    c_c = lax.complex(c_re.astype(f32), c_im.astype(f32))
    y = (jnp.einsum('ghp,blgp->blgh', c_c[0], h_f).real
         + jnp.einsum('ghp,blgp->blgh', c_c[1], h_b).real
         + d_skip.astype(f32) * u)
    z = jax.nn.gelu(y.reshape(bsz, L, S5_WIDTH))
    val, gate = jnp.split(z @ w_glu.astype(f32) + b_glu.astype(f32), 2, axis=-1)
    return val * jax.nn.sigmoid(gate)


def gla_chunked(q, k, v, log_a, strict):
    bsz, nh, L, dk = q.shape
    dv = v.shape[-1]
    c = GLA_CHUNK
    n = L // c
    q, k, log_a = [t.reshape(bsz, nh, n, c, dk) for t in (q, k, log_a)]
    v = v.reshape(bsz, nh, n, c, dv)
    b = jnp.cumsum(log_a, axis=3)
    b_last = b[:, :, :, -1:]
    q_in = q * jnp.exp(b)
    k_in = k * jnp.exp(-b)
    k_st = k * jnp.exp(b_last - b)
    scores = jnp.einsum('bhnid,bhnjd->bhnij', q_in, k_in)
    mask = jnp.tril(jnp.ones((c, c), dtype=bool), k=-1 if strict else 0)
    o_intra = jnp.einsum('bhnij,bhnjv->bhniv', jnp.where(mask, scores, 0.0), v)
    kv = jnp.einsum('bhnjd,bhnjv->bhndv', k_st, v)
    decay = jnp.exp(b_last[:, :, :, 0])

    def step(state, inp):
        kv_n, dec_n = inp
        return dec_n[..., None] * state + kv_n, state

    init = jnp.zeros((bsz, nh, dk, dv), q.dtype)
    _, states = lax.scan(step, init, (jnp.moveaxis(kv, 2, 0), jnp.moveaxis(decay, 2, 0)))
    states = jnp.moveaxis(states, 0, 2)
    o_inter = jnp.einsum('bhnid,bhndv->bhniv', q_in, states)
    return (o_intra + o_inter).reshape(bsz, nh, L, dv)


def gla_mixer(q, k, v, r, lr_f, lr_b, w_a, b_a, ln_g):
    f32 = jnp.float32
    bsz, L, _ = q.shape

    def heads(t, d):
        return t.astype(f32).reshape(bsz, L, GLA_HEADS, d).transpose(0, 2, 1, 3)

    q = heads(q, GLA_DK) * (GLA_DK ** -0.5)
    k = heads(k, GLA_DK)
    v = heads(v, GLA_DV)
    w_a = w_a.astype(f32)
    b_a = b_a.astype(f32)
    la_f = heads(jax.nn.log_sigmoid(lr_f.astype(f32) @ w_a[0] + b_a[0]) / GLA_TAU, GLA_DK)
    la_b = heads(jax.nn.log_sigmoid(lr_b.astype(f32) @ w_a[1] + b_a[1]) / GLA_TAU, GLA_DK)
    flip = lambda t: jnp.flip(t, axis=2)
    o = (gla_chunked(q, k, v, la_f, strict=False)
         + flip(gla_chunked(flip(q), flip(k), flip(v), flip(la_b), strict=True)))
    mu = jnp.mean(o, axis=-1, keepdims=True)
    var = jnp.mean(jnp.square(o - mu), axis=-1, keepdims=True)
    o = ((o - mu) * lax.rsqrt(var + LN_EPS)).transpose(0, 2, 1, 3).reshape(bsz, L, GLA_WIDTH)
    return o * ln_g.astype(f32) * jax.nn.silu(r.astype(f32))


def rope_partial(t, cos, sin):
    rot, rest = t[..., :ROT_DIM], t[..., ROT_DIM:]
    x1, x2 = rot[..., :ROT_DIM // 2], rot[..., ROT_DIM // 2:]
    rotated = jnp.concatenate([x1 * cos - x2 * sin, x2 * cos + x1 * sin], axis=-1)
    return jnp.concatenate([rotated.astype(t.dtype), rest], axis=-1)


def swa_mixer(q, k, v, sink):
    f32 = jnp.float32
    bsz, L, _ = q.shape
    nb = L // SWA_BLOCK
    grp = SWA_HEADS // SWA_KV_HEADS
    q = q.reshape(bsz, L, SWA_HEADS, SWA_HEAD_DIM)
    k = k.reshape(bsz, L, SWA_KV_HEADS, SWA_HEAD_DIM)
    v = v.reshape(bsz, L, SWA_KV_HEADS, SWA_HEAD_DIM)
    pos = jnp.arange(L, dtype=f32)
    inv_freq = ROPE_THETA ** (-jnp.arange(0, ROT_DIM, 2, dtype=f32) / ROT_DIM)
    ang = pos[:, None] * inv_freq[None, :]
    cos, sin = jnp.cos(ang)[:, None, :], jnp.sin(ang)[:, None, :]
    q = rope_partial(q, cos, sin)
    k = rope_partial(k, cos, sin)

    def band(t):
        tp = jnp.pad(t, ((0, 0), (SWA_BLOCK, SWA_BLOCK), (0, 0), (0, 0)))
        tp = tp.reshape(bsz, nb + 2, SWA_BLOCK, SWA_KV_HEADS, SWA_HEAD_DIM)
        return jnp.concatenate([tp[:, :-2], tp[:, 1:-1], tp[:, 2:]], axis=2)

    kb, vb = band(k), band(v)
    qb = q.reshape(bsz, nb, SWA_BLOCK, SWA_KV_HEADS, grp, SWA_HEAD_DIM)
    s = jnp.einsum('bnqhgd,bnkhd->bnhgqk', qb, kb).astype(f32) * (SWA_HEAD_DIM ** -0.5)
    blk = jnp.arange(nb)[:, None] * SWA_BLOCK
    qpos = blk + jnp.arange(SWA_BLOCK)[None, :]
    kpos = blk - SWA_BLOCK + jnp.arange(3 * SWA_BLOCK)[None, :]
    valid = ((jnp.abs(qpos[:, :, None] - kpos[:, None, :]) <= SWA_WINDOW)
             & (kpos >= 0)[:, None, :] & (kpos < L)[:, None, :])
    s = jnp.where(valid[None, :, None, None], s, NEG_BIG)
    sink_col = jnp.broadcast_to(sink.astype(f32).reshape(1, 1, SWA_KV_HEADS, grp, 1, 1),
                                s.shape[:-1] + (1,))
    p = jax.nn.softmax(jnp.concatenate([s, sink_col], axis=-1), axis=-1)[..., :-1]
    o = jnp.einsum('bnhgqk,bnkhd->bnqhgd', p.astype(vb.dtype), vb)
    return o.reshape(bsz, L, SWA_WIDTH)


def hybrid_mixer(x, w_in, a_re, a_im, log_step, b_re, b_im, c_re, c_im, d_skip,
                 w_glu, b_glu, gla_w_a, gla_b_a, gla_ln_g, swa_sink, w_out):
    h = x @ w_in
    points = np.cumsum(IN_SIZES)[:-1].tolist()
    (s5_u, g_q, g_k, g_v, g_r, g_lf, g_lb, a_q, a_k, a_v) = jnp.split(h, points, axis=-1)
    y_a = s5_mixer(s5_u, a_re, a_im, log_step, b_re, b_im, c_re, c_im, d_skip, w_glu, b_glu)
    y_b = gla_mixer(g_q, g_k, g_v, g_r, g_lf, g_lb, gla_w_a, gla_b_a, gla_ln_g)
    y_c = swa_mixer(a_q, a_k, a_v, swa_sink)
    y = jnp.concatenate([y_a.astype(x.dtype), y_b.astype(x.dtype), y_c.astype(x.dtype)], axis=-1)
    return y @ w_out


def setup_inputs(seed: int = 0) -> dict:
    key = jax.random.key(seed)
    ks = jax.random.split(key, 24)
    f32 = jnp.float32
    nrm = lambda k, shape, scale: jax.random.normal(k, shape, f32) * scale
    L2 = (DEPTH, 2, S5_GROUPS, S5_STATE)
    x = nrm(ks[0], (BATCH, SEQ, D_MODEL), 1.0)
    w_in = nrm(ks[1], (DEPTH, D_MODEL, D_IN), D_MODEL ** -0.5)
    s5_a_re = -0.5 + nrm(ks[2], L2, 0.01)
    s5_a_im = math.pi * jnp.arange(S5_STATE, dtype=f32) + nrm(ks[3], L2, 0.01)
    s5_log_step = jax.random.uniform(ks[4], L2, f32, math.log(1e-3), math.log(1e-1))
    s5_b_re = nrm(ks[5], L2 + (S5_GROUP,), (2 * S5_GROUP) ** -0.5)
    s5_b_im = nrm(ks[6], L2 + (S5_GROUP,), (2 * S5_GROUP) ** -0.5)
    s5_c_re = nrm(ks[7], (DEPTH, 2, S5_GROUPS, S5_GROUP, S5_STATE), S5_STATE ** -0.5)
    s5_c_im = nrm(ks[8], (DEPTH, 2, S5_GROUPS, S5_GROUP, S5_STATE), S5_STATE ** -0.5)
    s5_d = nrm(ks[9], (DEPTH, S5_GROUPS, S5_GROUP), 1.0)
    s5_w_glu = nrm(ks[10], (DEPTH, S5_WIDTH, 2 * S5_WIDTH), S5_WIDTH ** -0.5)
    s5_b_glu = nrm(ks[11], (DEPTH, 2 * S5_WIDTH), 0.01)
    gla_w_a = nrm(ks[12], (DEPTH, 2, GLA_RANK, GLA_QK), GLA_RANK ** -0.5)
    gla_b_a = nrm(ks[13], (DEPTH, 2, GLA_QK), 0.01)
    gla_ln_g = 1.0 + nrm(ks[14], (DEPTH, GLA_WIDTH), 0.02)
    swa_sink = nrm(ks[15], (DEPTH, SWA_HEADS), 0.5)
    w_out = nrm(ks[16], (DEPTH, D_MODEL, D_MODEL), D_MODEL ** -0.5 * DEEPNORM_BETA)
    ln1_g = 1.0 + nrm(ks[17], (DEPTH, D_MODEL), 0.02)
    ln1_b = nrm(ks[18], (DEPTH, D_MODEL), 0.01)
    w_ff1 = nrm(ks[19], (DEPTH, D_MODEL, D_FF), D_MODEL ** -0.5)
    w_ff2 = nrm(ks[20], (DEPTH, D_FF, D_MODEL), D_FF ** -0.5 * DEEPNORM_BETA)
    ln2_g = 1.0 + nrm(ks[21], (DEPTH, D_MODEL), 0.02)
    ln2_b = nrm(ks[22], (DEPTH, D_MODEL), 0.01)
    return {"x": x, "w_in": w_in, "s5_a_re": s5_a_re, "s5_a_im": s5_a_im,
            "s5_log_step": s5_log_step, "s5_b_re": s5_b_re, "s5_b_im": s5_b_im,
            "s5_c_re": s5_c_re, "s5_c_im": s5_c_im, "s5_d": s5_d,
            "s5_w_glu": s5_w_glu, "s5_b_glu": s5_b_glu, "gla_w_a": gla_w_a,
            "gla_b_a": gla_b_a, "gla_ln_g": gla_ln_g, "swa_sink": swa_sink,
            "w_out": w_out, "ln1_g": ln1_g, "ln1_b": ln1_b, "w_ff1": w_ff1,
            "w_ff2": w_ff2, "ln2_g": ln2_g, "ln2_b": ln2_b}


def reference(x, w_in, s5_a_re, s5_a_im, s5_log_step, s5_b_re, s5_b_im, s5_c_re, s5_c_im,
              s5_d, s5_w_glu, s5_b_glu, gla_w_a, gla_b_a, gla_ln_g, swa_sink, w_out,
              ln1_g, ln1_b, w_ff1, w_ff2, ln2_g, ln2_b):
    for l in range(DEPTH):
        mix = hybrid_mixer(x, w_in[l], s5_a_re[l], s5_a_im[l], s5_log_step[l],
                           s5_b_re[l], s5_b_im[l], s5_c_re[l], s5_c_im[l], s5_d[l],
                           s5_w_glu[l], s5_b_glu[l], gla_w_a[l], gla_b_a[l], gla_ln_g[l],
                           swa_sink[l], w_out[l])
        x = layer_norm(DEEPNORM_ALPHA * x + mix, ln1_g[l], ln1_b[l])
        hid = jnp.square(jax.nn.relu(x @ w_ff1[l]))
        x = layer_norm(DEEPNORM_ALPHA * x + hid @ w_ff2[l], ln2_g[l], ln2_b[l])
    return x
```

```python
import math
import os
PIECES = os.environ.get('S5PIECES', 'tbuild,tgasm,gint,gproj').split(',')
from contextlib import ExitStack

import numpy as np
import concourse.bass as bass
import concourse.mybir as mybir
from concourse.bass_utils import run_bass_kernel_spmd

F32 = mybir.dt.float32
BF16 = mybir.dt.bfloat16
I32 = mybir.dt.int32
AF = mybir.ActivationFunctionType
ALU = mybir.AluOpType
AX = mybir.AxisListType

NCORES = 8
D = 1024
L = 2048
NSEQ = 2
NTOK = NSEQ * L
DEPTH = 2
D_IN = 1824
D_FF = 4096
ALPHA = (2 * DEPTH) ** 0.25
LN_EPS = 1e-5
TM_COLS = 1408
C_S5, C_GK, C_GV, C_AQ, C_AK, C_AV = 0, 256, 384, 640, 1152, 1280


class Cx:
    ENGS = ("pe", "act", "dve", "pool", "sp")

    def __init__(self, nc):
        self.nc = nc
        self.eng = {"pe": nc.tensor, "act": nc.scalar, "dve": nc.vector, "pool": nc.gpsimd, "sp": nc.sync}
        self.sem = {e: nc.alloc_semaphore("s_" + e) for e in self.ENGS}
        self.cnt = {e: 0 for e in self.ENGS}
        self.waited = {e: {} for e in self.ENGS}
        self.last_w = {}
        self.readers = {}
        self.ndsem = 12
        self.dsem = {q: [nc.alloc_semaphore(f"d_{q}{i}") for i in range(self.ndsem)] for q in ("sp", "pool", "act")}
        self.dval = {q: [0] * self.ndsem for q in ("sp", "pool", "act")}
        self.dnext = {q: 0 for q in ("sp", "pool", "act")}
        self.dma_tokens = []
        self.ninst = 0

    def _wait(self, e, tok):
        if tok is None:
            return
        kind = tok[0]
        if kind == "c":
            _, src, c = tok
            if src == e:
                if e == "pe":
                    return
                if c > self.cnt[e]:
                    return
            if self.waited[e].get(src, 0) >= c:
                return
            self.eng[e].wait_ge(self.sem[src], c)
            self.waited[e][src] = c
        else:
            _, q, i, v = tok
            key = ("d", q, i)
            if self.waited[e].get(key, 0) >= v:
                return
            self.eng[e].wait_ge(self.dsem[q][i], v)
            self.waited[e][key] = v

    def _deps(self, reads, writes):
        deps = []
        for k in reads:
            t = self.last_w.get(k)
            if t is not None:
                deps.append(t)
        for k in writes:
            t = self.last_w.get(k)
            if t is not None:
                deps.append(t)
            deps.extend(self.readers.get(k, ()))
        return deps

    def _update(self, tok, reads, writes):
        for k in reads:
            self.readers.setdefault(k, []).append(tok)
        for k in writes:
            self.last_w[k] = tok
            self.readers[k] = []

    def op(self, e, fn, reads=(), writes=(), sig=True, extra=()):
        for t in self._deps(reads, writes):
            self._wait(e, t)
        for t in extra:
            self._wait(e, t)
        ins = fn(self.eng[e])
        self.ninst += 1
        if sig:
            ins.then_inc(self.sem[e], 1)
            self.cnt[e] += 1
            tok = ("c", e, self.cnt[e])
        else:
            tok = ("c", e, self.cnt[e] + 1)
        self._update(tok, reads, writes)
        return tok

    def dma(self, q, out, in_, reads=(), writes=(), **kw):
        i = self.dnext[q]
        self.dnext[q] = (i + 1) % self.ndsem
        if self.dval[q][i] > 0:
            self._wait(q, ("d", q, i, self.dval[q][i]))
        for t in self._deps(reads, writes):
            self._wait(q, t)
        ins = self.eng[q].dma_start(out=out, in_=in_, **kw)
        self.ninst += 1
        self.dval[q][i] += 16
        ins.then_inc(self.dsem[q][i], 16)
        tok = ("d", q, i, self.dval[q][i])
        self._update(tok, reads, writes)
        self.dma_tokens.append(tok)
        return tok

    def barrier(self):
        for e in self.ENGS:
            for src in self.ENGS:
                if src != e and self.cnt[src] > 0:
                    self._wait(e, ("c", src, self.cnt[src]))
            for q in self.dsem:
                for i in range(self.ndsem):
                    if self.dval[q][i] > 0:
                        self._wait(e, ("d", q, i, self.dval[q][i]))
        self.last_w = {}
        self.readers = {}
        self.dma_tokens = []


class NCP:
    def __init__(self, nc):
        self._nc = nc
        self.suffix = ""

    def __getattr__(self, k):
        return getattr(self._nc, k)

    def sbuf_tensor(self, name, shape, dt, **kw):
        return self._nc.sbuf_tensor(name + self.suffix, shape, dt, **kw)

    def psum_tensor(self, name, shape, dt, **kw):
        return self._nc.psum_tensor(name + self.suffix, shape, dt, **kw)


def evac(cx, i, out, in_, reads, writes):
    if i % 2 == 0:
        return cx.op("act", lambda e: e.activation(out=out, in_=in_, func=AF.Copy), reads, writes)
    return cx.op("dve", lambda e: e.tensor_copy(out=out, in_=in_), reads, writes)


def mm(cx, out, lhsT, rhs, start, stop, reads, writes, sig=None):
    if sig is None:
        sig = stop
    return cx.op("pe", lambda e: e.matmul(out, lhsT=lhsT, rhs=rhs, start=start, stop=stop),
                 reads, writes, sig=sig)


def make_consts(cx, es):
    nc = cx.nc
    C = {}
    idf = es.enter_context(nc.sbuf_tensor("idf", [128, 128], F32))
    idb = es.enter_context(nc.sbuf_tensor("idb", [128, 128], BF16))
    cx.op("pool", lambda e: e.memset(idf[:], 1.0), (), ["idf"])
    cx.op("pool", lambda e: e.affine_select(out=idf[:], in_=idf[:], pattern=[[-1, 128]], compare_op=ALU.is_equal,
                                            fill=0.0, base=0, channel_multiplier=1), ["idf"], ["idf"])
    cx.op("pool", lambda e: e.tensor_copy(out=idb[:], in_=idf[:]), ["idf"], ["idb"])
    C["idf"], C["idb"] = idf, idb
    return C


def phase_A(cx, C, X, w_in, S):
    nc = cx.nc
    with ExitStack() as es:
        wi = es.enter_context(nc.sbuf_tensor("wi", [128, 8, D_IN], BF16))
        xbf = [es.enter_context(nc.sbuf_tensor(f"xbf{i}", [128, 4, D], BF16)) for i in range(2)]
        xT = [es.enter_context(nc.sbuf_tensor(f"xT{i}", [128, 8, 512], BF16)) for i in range(2)]
        tm = [es.enter_context(nc.sbuf_tensor(f"tm{i}", [128, 4, TM_COLS], BF16)) for i in range(2)]
        fm = [es.enter_context(nc.sbuf_tensor(f"fm{i}", [128, 4, 512], BF16)) for i in range(2)]
        lr = [es.enter_context(nc.sbuf_tensor(f"lr{i}", [32, 512], F32)) for i in range(2)]
        pT = [es.enter_context(nc.psum_tensor(f"pT{i}", [128, 512], BF16)) for i in range(2)]
        pM = [es.enter_context(nc.psum_tensor(f"pM{i}", [128, 512], F32)) for i in range(4)]
        for k in range(8):
            cx.dma("pool", wi[:, k, :], w_in[k * 128:(k + 1) * 128, :], (), [("wi", k)])
        nblk = NTOK // 512
        ev = 1
        pm_i = 0
        for tb in range(nblk):
            s = tb % 2
            cx.dma("pool", xbf[s][:], X[tb * 512:(tb + 1) * 512, :].rearrange("(m p) d -> p m d", p=128),
                   (), [("xbf", s)])
            for k in range(8):
                pt = pT[k % 2]
                for m in range(4):
                    cx.op("pe", lambda e, pt=pt, m=m, k=k: e.transpose(pt[:, m * 128:(m + 1) * 128],
                                                                      xbf[s][:, m, k * 128:(k + 1) * 128], C["idb"][:]),
                          [("xbf", s), "idb"], [("pT", k % 2)], sig=(m == 3))
                evac(cx, ev, xT[s][:, k, :], pt[:], [("pT", k % 2)], [("xT", s, k)])
                ev += 1
            groups = [(0, 256, C_S5), (384, 768, C_GK), (1056, 1568, C_AQ), (1568, 1824, C_AK)]
            for m in range(4):
                for (c0, c1, dst) in groups:
                    ps = pM[pm_i % 4]
                    pk = ("pM", pm_i % 4)
                    pm_i += 1
                    w = c1 - c0
                    for k in range(8):
                        mm(cx, ps[:, 0:w], xT[s][:, k, m * 128:(m + 1) * 128], wi[:, k, c0:c1], k == 0, k == 7,
                           [("xT", s, k), ("wi", k)], [pk])
                    evac(cx, ev, tm[s][:, m, dst:dst + w], ps[:, 0:w], [pk], [("tm", s, m, dst)])
                    ev += 1
            cx.dma("sp", S["HT"][tb * 512:(tb + 1) * 512, :].rearrange("(m p) c -> p m c", p=128), tm[s][:],
                   [("tm", s, m, g[2]) for m in range(4) for g in groups], [("HT", tb)])
            for j, c0 in enumerate((256, 384, 768, 896)):
                ps = pM[pm_i % 4]
                pk = ("pM", pm_i % 4)
                pm_i += 1
                for k in range(8):
                    mm(cx, ps[:, :], wi[:, k, c0:c0 + 128], xT[s][:, k, :], k == 0, k == 7,
                       [("xT", s, k), ("wi", k)], [pk])
                evac(cx, ev, fm[s][:, j, :], ps[:, :], [pk], [("fm", s, j)])
                ev += 1
            cx.dma("sp", S["HF"][:, :, tb * 512:(tb + 1) * 512].rearrange("j p t -> p j t"), fm[s][:],
                   [("fm", s, j) for j in range(4)], [("HF", tb)])
            ps = pM[pm_i % 4]
            pk = ("pM", pm_i % 4)
            pm_i += 1
            for k in range(8):
                mm(cx, ps[0:32, :], wi[:, k, 1024:1056], xT[s][:, k, :], k == 0, k == 7,
                   [("xT", s, k), ("wi", k)], [pk])
            cx.op("dve", lambda e: e.tensor_copy(out=lr[s][:], in_=ps[0:32, :]), [pk], [("lr", s)])
            cx.dma("sp", S["HLR"][:, tb * 512:(tb + 1) * 512], lr[s][:], [("lr", s)], [("HLR", tb)])
        cx.barrier()


def make_swa_consts(cx, es, C):
    nc = cx.nc
    CC = es.enter_context(nc.sbuf_tensor("ropeC", [128, 16, 16], F32))
    SS = es.enter_context(nc.sbuf_tensor("ropeS", [128, 16, 16], F32))
    mL = es.enter_context(nc.sbuf_tensor("maskL", [128, 4, 128], BF16))
    mU = es.enter_context(nc.sbuf_tensor("maskU", [128, 4, 128], BF16))
    ones = es.enter_context(nc.sbuf_tensor("ones64", [128, 64], BF16))
    with ExitStack() as tmp:
        posi = tmp.enter_context(nc.sbuf_tensor("posi", [128, 16], I32))
        posf = tmp.enter_context(nc.sbuf_tensor("posf", [128, 16], F32))
        yv = tmp.enter_context(nc.sbuf_tensor("ropey", [128, 2, 16, 8], F32))
        yi = tmp.enter_context(nc.sbuf_tensor("ropeyi", [128, 2, 16, 8], I32))
        yr = tmp.enter_context(nc.sbuf_tensor("ropeyr", [128, 2, 16, 8], F32))
        fl = tmp.enter_context(nc.sbuf_tensor("ropefl", [128, 2, 16, 8], F32))
        cx.op("pool", lambda e: e.iota(posi[:], pattern=[[128, 16]], base=0, channel_multiplier=1), (), ["posi"])
        cx.op("dve", lambda e: e.tensor_copy(out=posf[:], in_=posi[:]), ["posi"], ["posf"])
        for j in range(8):
            f = (500000.0 ** (-(2 * j) / 16.0)) / (2 * math.pi)
            cx.op("dve", lambda e, j=j, f=f: e.tensor_scalar(out=yv[:, 0, :, j], in0=posf[:], scalar1=f, scalar2=None,
                                                            op0=ALU.mult), ["posf"], ["yv"])
            cx.op("dve", lambda e, j=j, f=f: e.tensor_scalar(out=yv[:, 1, :, j], in0=posf[:], scalar1=f, scalar2=0.25,
                                                            op0=ALU.mult, op1=ALU.add), ["posf"], ["yv"])
        cx.op("dve", lambda e: e.tensor_copy(out=yi[:], in_=yv[:]), ["yv"], ["yi"])
        cx.op("dve", lambda e: e.tensor_copy(out=yr[:], in_=yi[:]), ["yi"], ["yr"])
        cx.op("dve", lambda e: e.tensor_tensor(out=yr[:], in0=yv[:], in1=yr[:], op=ALU.subtract), ["yv", "yr"], ["yr"])
        cx.op("dve", lambda e: e.tensor_scalar(out=fl[:], in0=yr[:], scalar1=0.5, scalar2=None, op0=ALU.is_gt),
              ["yr"], ["fl"])
        cx.op("dve", lambda e: e.tensor_tensor(out=yr[:], in0=yr[:], in1=fl[:], op=ALU.subtract), ["yr", "fl"], ["yr"])
        cx.op("dve", lambda e: e.tensor_scalar(out=fl[:], in0=yr[:], scalar1=-0.5, scalar2=None, op0=ALU.is_lt),
              ["yr"], ["fl"])
        cx.op("dve", lambda e: e.tensor_tensor(out=yr[:], in0=yr[:], in1=fl[:], op=ALU.add), ["yr", "fl"], ["yr"])
        sc = 2 * math.pi * (1.0 - 2e-6)
        for half in range(2):
            cx.op("act", lambda e, half=half: e.activation(out=SS[:, :, half * 8:(half + 1) * 8], in_=yr[:, 0, :, :],
                                                            func=AF.Sin, scale=sc), ["yr"], ["ropeS"])
            cx.op("act", lambda e, half=half: e.activation(out=CC[:, :, half * 8:(half + 1) * 8], in_=yr[:, 1, :, :],
                                                            func=AF.Sin, scale=sc), ["yr"], ["ropeC"])
        cx.op("pool", lambda e: e.memset(mL[:], 1.0), (), ["maskL"])
        cx.op("pool", lambda e: e.memset(mU[:], 1.0), (), ["maskU"])
        cx.op("pool", lambda e: e.memset(ones[:], 1.0), (), ["ones64"])
        cx.op("pool", lambda e: e.affine_select(out=mL[:], in_=mL[:], pattern=[[0, 4], [-1, 128]], compare_op=ALU.is_ge,
                                                fill=0.0, base=0, channel_multiplier=1), ["maskL"], ["maskL"])
        cx.op("pool", lambda e: e.affine_select(out=mU[:], in_=mU[:], pattern=[[0, 4], [1, 128]], compare_op=ALU.is_ge,
                                                fill=0.0, base=0, channel_multiplier=-1), ["maskU"], ["maskU"])
        cx.barrier()
    C.update(ropeC=CC, ropeS=SS, maskL=mL, maskU=mU, ones64=ones)


def phase_D(cx, C, S, sink):
    nc = cx.nc
    NB = L // 128
    with ExitStack() as es:
        vtok = es.enter_context(nc.sbuf_tensor("vtok", [128, NB, 128], BF16))
        qk = es.enter_context(nc.sbuf_tensor("qk", [128, NB, 640], BF16))
        tAC = es.enter_context(nc.sbuf_tensor("tAC", [128, NB, 10, 16], F32))
        tDB = es.enter_context(nc.sbuf_tensor("tDB", [128, NB, 10, 16], F32))
        QT = es.enter_context(nc.sbuf_tensor("QT", [64, NB, 8, 128], BF16))
        KT = es.enter_context(nc.sbuf_tensor("KT", [64, 2, L], BF16))
        es8 = es.enter_context(nc.sbuf_tensor("es8", [64, 8], F32))
        esB = es.enter_context(nc.sbuf_tensor("esB", [64, 8, 128], F32))
        pt = [es.enter_context(nc.sbuf_tensor(f"pt{i}", [128, 512], BF16)) for i in range(6)]
        dn = [es.enter_context(nc.sbuf_tensor(f"dn{i}", [64, 512], F32)) for i in range(2)]
        rc = [es.enter_context(nc.sbuf_tensor(f"rc{i}", [64, 512], F32)) for i in range(2)]
        yc = [es.enter_context(nc.sbuf_tensor(f"yc{i}", [64, 8, 512], BF16)) for i in range(2)]
        pTr = [es.enter_context(nc.psum_tensor(f"pTr{i}", [64, 512], BF16)) for i in range(2)]
        pS = [es.enter_context(nc.psum_tensor(f"pS{i}", [128, 512], F32)) for i in range(3)]
        pO = [es.enter_context(nc.psum_tensor(f"pO{i}", [64, 512], F32)) for i in range(2)]
        pD = [es.enter_context(nc.psum_tensor(f"pD{i}", [64, 512], F32)) for i in range(1)]
        cx.dma("sp", es8[:], sink.partition_broadcast(64), (), ["es8"])
        cx.op("act", lambda e: e.activation(out=es8[:], in_=es8[:], func=AF.Exp), ["es8"], ["es8"])
        cx.op("dve", lambda e: e.tensor_copy(out=esB[:], in_=es8[:, :].unsqueeze(2).broadcast_to([64, 8, 128])),
              ["es8"], ["esB"])
        ev = 0
        pti = 0
        for sq in range(NSEQ):
            t0 = sq * L
            cx.dma("sp", vtok[:], S["HT"][t0:t0 + L, C_AV:C_AV + 128].rearrange("(b p) c -> p b c", p=128),
                   (), ["vtok"])
            cx.dma("sp", qk[:], S["HT"][t0:t0 + L, C_AQ:C_AQ + 640].rearrange("(b p) c -> p b c", p=128),
                   (), ["qk"])
            X = qk[:, :, :].rearrange("p b (h d) -> p b h d", d=64)
            Cb = C["ropeC"][:, :, :].unsqueeze(2).broadcast_to([128, NB, 10, 16])
            Sb = C["ropeS"][:, :, :].unsqueeze(2).broadcast_to([128, NB, 10, 16])
            cx.op("dve", lambda e: e.tensor_tensor(out=tAC[:], in0=X[:, :, :, 0:16], in1=Cb, op=ALU.mult),
                  ["qk"], ["tAC"])
            cx.op("pool", lambda e: e.tensor_tensor(out=tDB[:], in0=X[:, :, :, 0:16], in1=Sb, op=ALU.mult),
                  ["qk"], ["tDB"])
            cx.op("dve", lambda e: e.tensor_tensor(out=X[:, :, :, 0:8], in0=tAC[:, :, :, 0:8], in1=tDB[:, :, :, 8:16],
                                                  op=ALU.subtract), ["tAC", "tDB"], ["qk"])
            cx.op("dve", lambda e: e.tensor_tensor(out=X[:, :, :, 8:16], in0=tAC[:, :, :, 8:16], in1=tDB[:, :, :, 0:8],
                                                  op=ALU.add), ["tAC", "tDB"], ["qk"])
            for b in range(NB):
                for g in range(3):
                    p = pTr[ev % 2]
                    pk = ("pTr", ev % 2)
                    nh = 4 if g < 2 else 2
                    for hh in range(nh):
                        h = g * 4 + hh
                        cx.op("pe", lambda e, p=p, hh=hh, h=h, b=b: e.transpose(
                            p[:, hh * 128:(hh + 1) * 128], qk[:, b, h * 64:(h + 1) * 64], C["idb"][:]),
                            ["qk", "idb"], [pk], sig=(hh == nh - 1))
                    if g < 2:
                        evac(cx, ev, QT[:, b, g * 4:(g + 1) * 4, :],
                             p[:, :].rearrange("d (h t) -> d h t", t=128), [pk], [("QT", b)])
                    else:
                        evac(cx, ev, KT[:, :, b * 128:(b + 1) * 128],
                             p[:, 0:256].rearrange("d (h t) -> d h t", t=128), [pk], [("KT", b)])
                    ev += 1
            for n in range(NB):
                ys = (n // 4) % 2
                for kvh in range(2):
                    kbs = [kb for kb in (n - 1, n, n + 1) if 0 <= kb < NB]
                    pts = []
                    for kb in kbs:
                        ps = pS[pti % 3]
                        pk = ("pS", pti % 3)
                        P = pt[pti % 6]
                        ptk = ("pt", pti % 6)
                        pti += 1
                        mm(cx, ps[:, :], KT[:, kvh, kb * 128:(kb + 1) * 128],
                           QT[:, n, kvh * 4:(kvh + 1) * 4, :], True, True, [("KT", kb), ("QT", n)], [pk])
                        cx.op("act", lambda e, P=P, ps=ps: e.activation(out=P[:], in_=ps[:], func=AF.Exp, scale=0.125),
                              [pk], [ptk])
                        if kb != n:
                            M = C["maskL"] if kb < n else C["maskU"]
                            cx.op("pool", lambda e, P=P, M=M: e.tensor_tensor(
                                out=P[:], in0=P[:], in1=M[:, :, :].rearrange("p h t -> p (h t)"), op=ALU.mult),
                                [ptk], [ptk])
                        pts.append((kb, P, ptk))
                    po = pO[(2 * n + kvh) % 2]
                    pok = ("pO", (2 * n + kvh) % 2)
                    pd = pD[0]
                    pdk = ("pD", 0)
                    for i, (kb, P, ptk) in enumerate(pts):
                        mm(cx, po[:, :], vtok[:, kb, kvh * 64:(kvh + 1) * 64], P[:], i == 0, i == len(pts) - 1,
                           ["vtok", ptk], [pok])
                    for i, (kb, P, ptk) in enumerate(pts):
                        mm(cx, pd[:, :], C["ones64"][:], P[:], i == 0, i == len(pts) - 1, [ptk], [pdk])
                    d_ = dn[kvh]
                    r_ = rc[kvh]
                    cx.op("dve", lambda e, d_=d_, pd=pd: e.tensor_tensor(
                        out=d_[:], in0=pd[:], in1=esB[:, kvh * 4:(kvh + 1) * 4, :].rearrange("p h t -> p (h t)"),
                        op=ALU.add), [pdk, "esB"], [("dn", kvh)])
                    cx.op("act", lambda e, d_=d_, r_=r_: e.activation(out=r_[:], in_=d_[:], func=AF.Ln),
                          [("dn", kvh)], [("rc", kvh)])
                    cx.op("act", lambda e, r_=r_: e.activation(out=r_[:], in_=r_[:], func=AF.Exp, scale=-1.0),
                          [("rc", kvh)], [("rc", kvh)])
                    cx.op("dve", lambda e, r_=r_, po=po: e.tensor_tensor(
                        out=yc[ys][:, kvh * 4:(kvh + 1) * 4, (n % 4) * 128:(n % 4 + 1) * 128],
                        in0=po[:, :].rearrange("d (h t) -> d h t", t=128),
                        in1=r_[:, :].rearrange("d (h t) -> d h t", t=128), op=ALU.mult),
                        [pok, ("rc", kvh)], [("yc", ys, n % 4, kvh)])
                if n % 4 == 3:
                    tb0 = t0 + (n - 3) * 128
                    cx.dma("sp", S["YT"][512:1024, tb0:tb0 + 512].rearrange("(h d) t -> d h t", d=64), yc[ys][:],
                           [("yc", ys, i, kv) for i in range(4) for kv in range(2)], [("YTc", sq, n)])
        cx.barrier()


def make_gla_consts(cx, es, C):
    nc = cx.nc
    names = ("U_le", "U_lt", "U_ge", "U_gt")
    U = {n: es.enter_context(nc.sbuf_tensor(n, [128, 128], F32)) for n in names}
    mGT = es.enter_context(nc.sbuf_tensor("maskGT", [128, 4, 128], BF16))
    HM = es.enter_context(nc.sbuf_tensor("HM", [128, 4, 128], BF16))
    BM = es.enter_context(nc.sbuf_tensor("BM", [128, 4, 64], F32))
    Avg = es.enter_context(nc.sbuf_tensor("Avg", [128, 2, 64], F32))
    ones1 = es.enter_context(nc.sbuf_tensor("ones1", [1, 128], F32))
    v = -1.0 / 16.0
    specs = {"U_le": ([[1, 128]], -1, 0, ALU.is_ge), "U_lt": ([[1, 128]], -1, 0, ALU.is_gt),
             "U_ge": ([[-1, 128]], 1, 0, ALU.is_ge), "U_gt": ([[-1, 128]], 1, 0, ALU.is_gt)}
    for n in names:
        pat, cm, base, cmp_ = specs[n]
        cx.op("pool", lambda e, n=n: e.memset(U[n][:], v), (), [n])
        cx.op("pool", lambda e, n=n, pat=pat, cm=cm, base=base, cmp_=cmp_: e.affine_select(
            out=U[n][:], in_=U[n][:], pattern=pat, compare_op=cmp_, fill=0.0, base=base, channel_multiplier=cm),
            [n], [n])
    cx.op("pool", lambda e: e.memset(mGT[:], 1.0), (), ["maskGT"])
    cx.op("pool", lambda e: e.affine_select(out=mGT[:], in_=mGT[:], pattern=[[0, 4], [-1, 128]], compare_op=ALU.is_gt,
                                            fill=0.0, base=0, channel_multiplier=1), ["maskGT"], ["maskGT"])
    for (T, nm, blk, nb, inner) in ((HM, "HM", 32, 4, 128), (BM, "BM", 32, 4, 64), (Avg, "Avg", 64, 2, 64)):
        val = 1.0 / 64.0 if nm == "Avg" else 1.0
        cx.op("pool", lambda e, T=T: e.memset(T[:], val), (), [nm])
        cx.op("pool", lambda e, T=T, blk=blk, nb=nb, inner=inner: e.affine_select(
            out=T[:], in_=T[:], pattern=[[-blk, nb], [0, inner]], compare_op=ALU.is_ge, fill=0.0, base=0,
            channel_multiplier=1), [nm], [nm])
        cx.op("pool", lambda e, T=T, blk=blk, nb=nb, inner=inner: e.affine_select(
            out=T[:], in_=T[:], pattern=[[blk, nb], [0, inner]], compare_op=ALU.is_ge, fill=0.0, base=blk - 1,
            channel_multiplier=-1), [nm], [nm])
    cx.op("pool", lambda e: e.memset(ones1[:], 1.0), (), ["ones1"])
    C.update(U)
    C.update(maskGT=mGT, HM=HM, BM=BM, Avg=Avg, ones1=ones1)


def phase_C(cx, C, S, w_a, b_a, ln_g):
    nc = cx.nc
    NCH = L // 128
    SC = 32.0 ** -0.5
    with ExitStack() as es:
        sb = lambda name, shape, dt: es.enter_context(nc.sbuf_tensor(name, shape, dt))
        qT = sb("g_qT", [128, NTOK], BF16)
        kT = sb("g_kT", [128, NTOK], BF16)
        rT = sb("g_rT", [128, 2, NTOK], BF16)
        ktok = sb("g_ktok", [128, NTOK // 128, 128], BF16)
        vtok = sb("g_vtok", [128, NTOK // 128, 256], BF16)
        lrT = [sb(f"g_lrT{z}", [16, L], F32) for z in range(2)]
        wa = [sb(f"g_wa{z}", [16, 128], F32) for z in range(2)]
        ba = [sb(f"g_ba{z}", [1, 128], F32) for z in range(2)]
        gcol = sb("g_gcol", [128, 2], F32)
        spl = sb("g_sp", [128, 2, NTOK // 128, 128], F32)
        oacc = sb("g_oacc", [128, NSEQ, 2, L], F32)
        Sst32 = [[sb(f"g_S32_{q}{z}", [128, 256], F32) for z in range(2)] for q in range(NSEQ)]
        Sst = [[sb(f"g_S_{q}{z}", [128, 256], BF16) for z in range(2)] for q in range(NSEQ)]
        NS = 4
        eA = [sb(f"g_eA{i}", [128, 128], F32) for i in range(NS)]
        eB = [sb(f"g_eB{i}", [128, 128], F32) for i in range(NS)]
        eC = [sb(f"g_eC{i}", [128, 128], F32) for i in range(NS)]
        eK = [sb(f"g_eK{i}", [128, 128], F32) for i in range(NS)]
        dcol = [sb(f"g_dc{i}", [128, 1], F32) for i in range(NS)]
        qd = [sb(f"g_qd{i}", [128, 128], BF16) for i in range(NS)]
        qd2 = [sb(f"g_qd2{i}", [128, 128], BF16) for i in range(NS)]
        Qb = [sb(f"g_Qb{i}", [128, 4, 128], BF16) for i in range(NS)]
        kd = [sb(f"g_kd{i}", [128, 128], BF16) for i in range(NS)]
        kst = [sb(f"g_kst{i}", [128, 128], BF16) for i in range(NS)]
        Wt = [sb(f"g_W{i}", [128, 4, 128], BF16) for i in range(NS)]
        Vp = [sb(f"g_Vp{i}", [128, 4, 128], BF16) for i in range(NS)]
        kvt = [sb(f"g_kvt{i}", [128, 256], F32) for i in range(NS)]
        o32 = [sb(f"g_o32{i}", [128, 2, 128], F32) for i in range(2)]
        cen = [sb(f"g_cen{i}", [128, 2, 128], F32) for i in range(2)]
        sq_ = [sb(f"g_sq{i}", [128, 2, 128], F32) for i in range(2)]
        rstd = [sb(f"g_rstd{i}", [128, 2, 128], F32) for i in range(2)]
        sil = [sb(f"g_sil{i}", [128, 2, 128], F32) for i in range(2)]
        yb = [sb(f"g_yb{i}", [128, 2, 128], BF16) for i in range(2)]
        pA = [es.enter_context(nc.psum_tensor(f"g_pA{i}", [128, 128], F32)) for i in range(3)]
        pSc = [es.enter_context(nc.psum_tensor(f"g_pSc{i}", [128, 512], F32)) for i in range(2)]
        pO = [es.enter_context(nc.psum_tensor(f"g_pO{i}", [128, 256], F32)) for i in range(2)]
        pKV = [es.enter_context(nc.psum_tensor(f"g_pKV{i}", [128, 256], F32)) for i in range(1)]
        cnt = {"pA": 0, "pSc": 0, "pO": 0}

        def rot(name, arr):
            i = cnt[name] % len(arr)
            cnt[name] += 1
            return arr[i], (name, i)

        cx.dma("sp", qT[:], S["HF"][0], (), ["qT"])
        cx.dma("sp", kT[:], S["HF"][1], (), ["kT"])
        cx.dma("sp", rT[:], S["HF"][2:4].rearrange("j p t -> p j t"), (), ["rT"])
        cx.dma("sp", ktok[:], S["HT"][:, C_GK:C_GK + 128].rearrange("(c p) f -> p c f", p=128), (), ["ktok"])
        cx.dma("sp", vtok[:], S["HT"][:, C_GV:C_GV + 256].rearrange("(c p) f -> p c f", p=128), (), ["vtok"])
        for z in range(2):
            cx.dma("sp", wa[z][:], w_a[z], (), [("wa", z)])
            cx.dma("sp", ba[z][:], b_a[z:z + 1, :], (), [("ba", z)])
        cx.dma("sp", gcol[:], ln_g.rearrange("(j p) -> p j", p=128), (), ["gcol"], allow_slow_non_contiguous=True)
        for i in range(NS):
            cx.op("pool", lambda e, i=i: e.memset(Vp[i][:], 0.0), (), [("Vp", i)])
        for q in range(NSEQ):
            for z in range(2):
                cx.op("pool", lambda e, q=q, z=z: e.memset(Sst32[q][z][:], 0.0), (), [("S32", q, z)])
                cx.op("pool", lambda e, q=q, z=z: e.memset(Sst[q][z][:], 0.0), (), [("S", q, z)])
        for c in range(NTOK // 128):
            if c % NCH == 0:
                for z in range(2):
                    cx.dma("sp", lrT[z][:], S["HLR"][16 * z:16 * z + 16, (c // NCH) * L:(c // NCH + 1) * L], (),
                           [("lrT", z)])
            for z in range(2):
                p, pk = rot("pA", pA)
                mm(cx, p[:, :], lrT[z][:, (c % NCH) * 128:(c % NCH + 1) * 128], wa[z][:], True, False,
                   [("lrT", z), ("wa", z)], [pk], sig=False)
                mm(cx, p[:, :], C["ones1"][:], ba[z][:], False, True, ["ones1", ("ba", z)], [pk])
                cx.op("act", lambda e, p=p, c=c, z=z: e.activation(out=spl[:, z, c, :], in_=p[:, :], func=AF.Exp,
                                                                  scale=-1.0), [pk], [("sp", z, c)])
                cx.op("act", lambda e, c=c, z=z: e.activation(out=spl[:, z, c, :], in_=spl[:, z, c, :], func=AF.Ln,
                                                              bias=1.0), [("sp", z, c)], [("sp", z, c)])

        fin = 0

        def visit(q, c, z, st):
            nonlocal fin
            gc = q * NCH + c
            tsl = slice(gc * 128, (gc + 1) * 128)
            spz = spl[:, z, gc, :]
            spk = ("sp", z, gc)
            pc1, pc1k = rot("pA", pA)
            mm(cx, pc1[:, :], spz, C["U_le" if z == 0 else "U_lt"][:], True, True, [spk], [pc1k])
            pc2, pc2k = rot("pA", pA)
            mm(cx, pc2[:, :], C["U_gt" if z == 0 else "U_lt"][:], spz, True, True, [spk], [pc2k])
            if z == 0:
                cx.op("act", lambda e: e.activation(out=eA[st][:], in_=pc1[:, :], func=AF.Exp, bias=math.log(SC)),
                      [pc1k], [("eA", st)])
                cx.op("act", lambda e: e.activation(out=eB[st][:], in_=pc1[:, :], func=AF.Exp, scale=-1.0),
                      [pc1k], [("eB", st)])
                cx.op("act", lambda e: e.activation(out=dcol[st][:], in_=pc1[:, 127:128], func=AF.Exp),
                      [pc1k], [("dcol", st)])
            else:
                pc3, pc3k = rot("pA", pA)
                mm(cx, pc3[:, :], spz, C["U_ge"][:], True, True, [spk], [pc3k])
                cx.op("act", lambda e: e.activation(out=eA[st][:], in_=pc1[:, :], func=AF.Exp, scale=-1.0,
                                                    bias=math.log(SC)), [pc1k], [("eA", st)])
                cx.op("act", lambda e: e.activation(out=eB[st][:], in_=pc1[:, :], func=AF.Exp), [pc1k], [("eB", st)])
                cx.op("act", lambda e: e.activation(out=eC[st][:], in_=pc3[:, :], func=AF.Exp, bias=math.log(SC)),
                      [pc3k], [("eC", st)])
                cx.op("act", lambda e: e.activation(out=dcol[st][:], in_=pc3[:, 0:1], func=AF.Exp),
                      [pc3k], [("dcol", st)])
            cx.op("act", lambda e: e.activation(out=eK[st][:], in_=pc2[:, :], func=AF.Exp), [pc2k], [("eK", st)])
            cx.op("dve", lambda e: e.tensor_tensor(out=qd[st][:], in0=qT[:, tsl], in1=eA[st][:], op=ALU.mult),
                  ["qT", ("eA", st)], [("qd", st)])
            cx.op("pool", lambda e: e.tensor_tensor(
                out=Qb[st][:], in0=qd[st][:, :].unsqueeze(1).broadcast_to([128, 4, 128]), in1=C["HM"][:],
                op=ALU.mult), [("qd", st)], [("Qb", st)])
            cx.op("dve", lambda e: e.tensor_tensor(out=kd[st][:], in0=kT[:, tsl], in1=eB[st][:], op=ALU.mult),
                  ["kT", ("eB", st)], [("kd", st)])
            cx.op("dve", lambda e: e.tensor_tensor(out=kst[st][:], in0=ktok[:, gc, :], in1=eK[st][:], op=ALU.mult),
                  ["ktok", ("eK", st)], [("kst", st)])
            if z == 0:
                qin, qink = qd[st], ("qd", st)
            else:
                cx.op("dve", lambda e: e.tensor_tensor(out=qd2[st][:], in0=qT[:, tsl], in1=eC[st][:], op=ALU.mult),
                      ["qT", ("eC", st)], [("qd2", st)])
                qin, qink = qd2[st], ("qd2", st)
            Vv = vtok[:, gc, :].rearrange("p (j hh v) -> p j hh v", hh=2, v=64)
            Vpv = Vp[st][:, :, :].rearrange("p (j hh) w -> p j hh w", hh=2)
            cx.op("pool", lambda e: e.tensor_copy(out=Vpv[:, :, 0, 0:64], in_=Vv[:, :, 0, :]),
                  ["vtok"], [("Vp", st)])
            cx.op("pool", lambda e: e.tensor_copy(out=Vpv[:, :, 1, 64:128], in_=Vv[:, :, 1, :]),
                  ["vtok"], [("Vp", st)])
            psc, psck = rot("pSc", pSc)
            mm(cx, psc[:, :], kd[st][:], Qb[st][:, :, :].rearrange("p h t -> p (h t)"), True, True,
               [("kd", st), ("Qb", st)], [psck])
            M = C["maskU"] if z == 0 else C["maskGT"]
            cx.op("dve", lambda e: e.tensor_tensor(out=Wt[st][:], in0=psc[:, :].rearrange("p (h t) -> p h t", t=128),
                                                  in1=M[:], op=ALU.mult), [psck], [("W", st)])
            po, pok = rot("pO", pO)
            for j in range(2):
                for hh in range(2):
                    h = 2 * j + hh
                    mm(cx, po[:, j * 128:(j + 1) * 128], Vp[st][:, h, :], Wt[st][:, h, :], hh == 0, False,
                       [("Vp", st), ("W", st)], [pok], sig=False)
                mm(cx, po[:, j * 128:(j + 1) * 128], Sst[q][z][:, j * 128:(j + 1) * 128], qin[:], False, True,
                   [("S", q, z), qink], [pok], sig=(j == 1))
            pkv, pkvk = pKV[0], ("pKV", 0)
            mm(cx, pkv[:, :], kst[st][:], vtok[:, gc, :], True, True, [("kst", st), "vtok"], [pkvk])
            cx.op("dve", lambda e: e.tensor_tensor(out=kvt[st][:], in0=pkv[:, :],
                                                  in1=C["BM"][:, :, :].rearrange("p h v -> p (h v)"), op=ALU.mult),
                  [pkvk], [("kvt", st)])
            cx.op("dve", lambda e: e.scalar_tensor_tensor(out=Sst32[q][z][:], in0=Sst32[q][z][:], scalar=dcol[st][:, 0:1],
                                                         in1=kvt[st][:], op0=ALU.mult, op1=ALU.add),
                  [("S32", q, z), ("dcol", st), ("kvt", st)], [("S32", q, z)])
            cx.op("act", lambda e: e.activation(out=Sst[q][z][:], in_=Sst32[q][z][:], func=AF.Copy),
                  [("S32", q, z)], [("S", q, z)])
            first = (z == 0 and c < NCH // 2) or (z == 1 and c >= NCH // 2)
            oa = oacc[:, q, :, c * 128:(c + 1) * 128]
            oak = ("oacc", q, c)
            pov = po[:, :].rearrange("p (j t) -> p j t", t=128)
            if first:
                cx.op("act", lambda e: e.activation(out=oa, in_=pov, func=AF.Copy), [pok], [oak])
                return
            f = fin % 2
            fin += 1
            cx.op("dve", lambda e: e.tensor_tensor(out=o32[f][:], in0=pov, in1=oa, op=ALU.add), [pok, oak], [("o32", f)])
            AvgM = C["Avg"][:, :, :].rearrange("p b v -> p (b v)")
            pm, pmk = rot("pO", pO)
            mm(cx, pm[:, :], AvgM, o32[f][:, :, :].rearrange("p j t -> p (j t)"), True, True, [("o32", f)], [pmk])
            cx.op("dve", lambda e: e.tensor_tensor(out=cen[f][:], in0=o32[f][:],
                                                  in1=pm[:, :].rearrange("p (j t) -> p j t", t=128), op=ALU.subtract),
                  [("o32", f), pmk], [("cen", f)])
            cx.op("act", lambda e: e.activation(out=sq_[f][:], in_=cen[f][:], func=AF.Square), [("cen", f)], [("sq", f)])
            pv, pvk = rot("pO", pO)
            mm(cx, pv[:, :], AvgM, sq_[f][:, :, :].rearrange("p j t -> p (j t)"), True, True, [("sq", f)], [pvk])
            cx.op("act", lambda e: e.activation(out=rstd[f][:], in_=pv[:, :].rearrange("p (j t) -> p j t", t=128),
                                                func=AF.Ln, bias=LN_EPS), [pvk], [("rstd", f)])
            cx.op("act", lambda e: e.activation(out=rstd[f][:], in_=rstd[f][:], func=AF.Exp, scale=-0.5),
                  [("rstd", f)], [("rstd", f)])
            cx.op("act", lambda e: e.activation(out=sil[f][:], in_=rT[:, :, tsl], func=AF.Silu), ["rT"], [("sil", f)])
            cx.op("dve", lambda e: e.tensor_tensor(out=cen[f][:], in0=cen[f][:], in1=rstd[f][:], op=ALU.mult),
                  [("cen", f), ("rstd", f)], [("cen", f)])
            cx.op("pool", lambda e: e.tensor_tensor(out=sil[f][:], in0=sil[f][:],
                                                   in1=gcol[:, :].unsqueeze(2).broadcast_to([128, 2, 128]),
                                                   op=ALU.mult), [("sil", f), "gcol"], [("sil", f)])
            cx.op("dve", lambda e: e.tensor_tensor(out=yb[f][:], in0=cen[f][:], in1=sil[f][:], op=ALU.mult),
                  [("cen", f), ("sil", f)], [("yb", f)])
            cx.dma("sp", S["YT"][256:512, tsl].rearrange("(j p) t -> p j t", p=128), yb[f][:], [("yb", f)],
                   [("YTb", gc)])

        for i in range(NCH):
            for q in range(NSEQ):
                visit(q, i, 0, 2 * q)
                visit(q, NCH - 1 - i, 1, 2 * q + 1)
        cx.barrier()


def make_s5_consts(cx, es, C):
    nc = cx.nc
    mF = es.enter_context(nc.sbuf_tensor("s5_mF", [128, 128], F32))
    mB = es.enter_context(nc.sbuf_tensor("s5_mB", [128, 128], F32))
    with ExitStack() as tmp:
        rep = tmp.enter_context(nc.sbuf_tensor("s5_rep", [8, 128], F32))
        bF = tmp.enter_context(nc.sbuf_tensor("s5_bF", [8, 128], F32))
        bB = tmp.enter_context(nc.sbuf_tensor("s5_bB", [8, 128], F32))
        pm = tmp.enter_context(nc.psum_tensor("s5_pm", [128, 128], F32))
        for T, nm in ((rep, "rep"), (bF, "bF"), (bB, "bB")):
            cx.op("pool", lambda e, T=T: e.memset(T[:], 1.0), (), [nm])
        cx.op("pool", lambda e: e.affine_select(out=rep[:], in_=rep[:], pattern=[[1, 128]], compare_op=ALU.is_ge,
                                                fill=0.0, base=0, channel_multiplier=-16), ["rep"], ["rep"])
        cx.op("pool", lambda e: e.affine_select(out=rep[:], in_=rep[:], pattern=[[-1, 128]], compare_op=ALU.is_ge,
                                                fill=0.0, base=15, channel_multiplier=16), ["rep"], ["rep"])
        cx.op("pool", lambda e: e.affine_select(out=bF[:], in_=bF[:], pattern=[[1, 128]], compare_op=ALU.is_ge,
                                                fill=0.0, base=0, channel_multiplier=-16), ["bF"], ["bF"])
        cx.op("pool", lambda e: e.affine_select(out=bB[:], in_=bB[:], pattern=[[-1, 128]], compare_op=ALU.is_ge,
                                                fill=0.0, base=15, channel_multiplier=16), ["bB"], ["bB"])
        mm(cx, pm[:, :], rep[:], bF[:], True, True, ["rep", "bF"], ["pm"])
        cx.op("dve", lambda e: e.tensor_copy(out=mF[:], in_=pm[:, :]), ["pm"], ["s5_mF"])
        mm(cx, pm[:, :], rep[:], bB[:], True, True, ["rep", "bB"], ["pm"])
        cx.op("dve", lambda e: e.tensor_copy(out=mB[:], in_=pm[:, :]), ["pm"], ["s5_mB"])
        cx.barrier()
    C.update(s5_mF=mF, s5_mB=mB)


def phase_B(cx, C, S, prm, bstop=None):
    nc = cx.nc
    CH = 32
    NCK = NTOK // CH
    NPS = L // CH
    with ExitStack() as es:
        sb = lambda name, shape, dt: es.enter_context(nc.sbuf_tensor(name, shape, dt))
        Ut = sb("b_Ut", [128, 16, 4, 128], BF16)
        Tg = sb("b_Tg", [128, 16, 7, 128], BF16)
        PW = [sb(f"b_PW{r}", [64, 32, 33], F32) for r in range(2)]
        RPW = [sb(f"b_RPW{r}", [64, 32, 32], F32) for r in range(2)]
        Cm = [sb(f"b_C{r}", [64, 32, 16], F32) for r in range(2)]
        Hc = [sb(f"b_H{r}", [64, 32, NCK], F32) for r in range(2)]
        lam32 = [sb(f"b_l32{r}", [64, 32], F32) for r in range(2)]
        dcolT = sb("b_dcol", [128, 16], F32)
        with ExitStack() as e1:
            sb1 = lambda name, shape, dt: e1.enter_context(nc.sbuf_tensor(name, shape, dt))
            Are, Aim, Lst = sb1("b_Are", [64, 32], F32), sb1("b_Aim", [64, 32], F32), sb1("b_Lst", [64, 32], F32)
            Bm = [sb1(f"b_B{r}", [64, 32, 16], F32) for r in range(2)]
            Cld = [sb1(f"b_Cld{r}", [128, 4, 64], F32) for r in range(2)]
            IPW = [sb1(f"b_IPW{r}", [64, 32, 8], F32) for r in range(2)]
            sq = [sb1(f"b_sq{r}", [64, 32], F32) for r in range(2)]
            sq2 = [sb1(f"b_sq2{r}", [64, 32], F32) for r in range(2)]
            wv = [sb1(f"b_w{r}", [64, 32], F32) for r in range(2)]
            t1 = sb1("b_t1", [64, 2112], F32)
            t2 = sb1("b_t2", [64, 2112], F32)
            Bb = [sb1(f"b_Bb{r}", [64, 32, 16], F32) for r in range(2)]
            pq = [e1.enter_context(nc.psum_tensor(f"b_pq{i}", [128, 512], F32)) for i in range(2)]
            pqb = [e1.enter_context(nc.psum_tensor(f"b_pqb{i}", [128, 512], BF16)) for i in range(2)]
            pg = [e1.enter_context(nc.psum_tensor(f"b_pg{i}", [64, 128], F32)) for i in range(2)]
            nsc = allow = dict(allow_slow_non_contiguous=True)
            cx.dma("sp", Are[:], prm["a_re"].rearrange("z g p -> p (z g)"), (), ["Are"], **nsc)
            cx.dma("sp", Aim[:], prm["a_im"].rearrange("z g p -> p (z g)"), (), ["Aim"], **nsc)
            cx.dma("sp", Lst[:], prm["log_step"].rearrange("z g p -> p (z g)"), (), ["Lst"], **nsc)
            for r, nm in ((0, "b_re"), (1, "b_im")):
                cx.dma("sp", Bm[r][:], prm[nm].rearrange("z g p h -> p (z g) h"), (), [("Bm", r)])
            for r, nm in ((0, "c_re"), (1, "c_im")):
                cx.dma("sp", Cld[r][:], prm[nm].rearrange("z g h p -> (z g h) p").rearrange("(i q) p -> q i p", q=128),
                       (), [("Cld", r)])
                for i in range(4):
                    mm_t = cx.op("pe", lambda e, r=r, i=i: e.transpose(pq[r][0:64, i * 128:(i + 1) * 128], Cld[r][:, i, :],
                                                                       C["idf"][:]), [("Cld", r), "idf"], [("pq", r)],
                                 sig=(i == 3))
                cx.op("dve", lambda e, r=r: e.tensor_copy(out=Cm[r][:, :, :].rearrange("p a h -> p (a h)"),
                                                         in_=pq[r][0:64, :]), [("pq", r)], [("Cm", r)])
            for s_lo in range(8):
                cx.dma("sp", dcolT[s_lo * 16:(s_lo + 1) * 16, :], prm["d"].rearrange("g h -> h g"), (), ["dcol"], **nsc)

            def tt(eng, out, a, b, op, rd, wr):
                return cx.op(eng, lambda e: e.tensor_tensor(out=out, in0=a, in1=b, op=op), rd, wr)

            def cmul(eng, outr, outi, ar, ai, br, bi, shp, rd, wr):
                n = 1
                for d_ in shp[1:]:
                    n *= d_
                if len(shp) == 2:
                    pat, kw = None, {}
                elif len(shp) == 3:
                    pat, kw = "p (a b) -> p a b", dict(b=shp[2])
                else:
                    pat, kw = "p (a b c) -> p a b c", dict(b=shp[2], c=shp[3])
                v1 = t1[0:shp[0], 0:n] if pat is None else t1[0:shp[0], 0:n].rearrange(pat, **kw)
                v2 = t2[0:shp[0], 0:n] if pat is None else t2[0:shp[0], 0:n].rearrange(pat, **kw)
                tt(eng, v1, ar, br, ALU.mult, rd, ["t1"])
                tt(eng, v2, ai, bi, ALU.mult, rd, ["t2"])
                tt(eng, outr, v1, v2, ALU.subtract, ["t1", "t2"], wr[0:1])
                tt(eng, v1, ar, bi, ALU.mult, rd, ["t1"])
                tt(eng, v2, ai, br, ALU.mult, rd, ["t2"])
                tt(eng, outi, v1, v2, ALU.add, ["t1", "t2"], wr[1:2])

            cx.op("act", lambda e: e.activation(out=Lst[:], in_=Lst[:], func=AF.Exp), ["Lst"], ["Lst"])
            tt("dve", wv[0][:], Are[:], Lst[:], ALU.mult, ["Are", "Lst"], ["wv0"])
            tt("dve", wv[1][:], Aim[:], Lst[:], ALU.mult, ["Aim", "Lst"], ["wv1"])
            cx.op("act", lambda e: e.activation(out=sq2[0][:], in_=wv[0][:], func=AF.Exp, scale=1.0 / 16), ["wv0"], ["e16"])
            cx.op("act", lambda e: e.activation(out=sq[1][:], in_=wv[1][:], func=AF.Sin, scale=1.0 / 16), ["wv1"], ["sq1"])
            cx.op("dve", lambda e: e.tensor_scalar(out=sq2[1][:], in0=wv[1][:], scalar1=1.0 / 16, scalar2=math.pi / 2,
                                                  op0=ALU.mult, op1=ALU.add), ["wv1"], ["ph"])
            cx.op("act", lambda e: e.activation(out=sq[0][:], in_=sq2[1][:], func=AF.Sin), ["ph"], ["sq0"])
            tt("dve", sq[0][:], sq[0][:], sq2[0][:], ALU.mult, ["sq0", "e16"], ["sq0"])
            tt("dve", sq[1][:], sq[1][:], sq2[0][:], ALU.mult, ["sq1", "e16"], ["sq1"])
            for it in range(4):
                cmul("dve", sq2[0][:], sq2[1][:], sq[0][:], sq[1][:], sq[0][:], sq[1][:], [64, 32],
                     ["sq0", "sq1"], ["sq20", "sq21"])
                cx.op("dve", lambda e: e.tensor_copy(out=sq[0][:], in_=sq2[0][:]), ["sq20"], ["sq0"])
                cx.op("dve", lambda e: e.tensor_copy(out=sq[1][:], in_=sq2[1][:]), ["sq21"], ["sq1"])
            cx.op("dve", lambda e: e.tensor_scalar(out=sq2[0][:], in0=sq[0][:], scalar1=-1.0, scalar2=None, op0=ALU.add),
                  ["sq0"], ["sq20"])
            tt("dve", wv[0][:], Are[:], Are[:], ALU.mult, ["Are"], ["wv0"])
            tt("dve", wv[1][:], Aim[:], Aim[:], ALU.mult, ["Aim"], ["wv1"])
            tt("dve", wv[0][:], wv[0][:], wv[1][:], ALU.add, ["wv0", "wv1"], ["wv0"])
            cx.op("dve", lambda e: e.reciprocal(out=wv[0][:], in_=wv[0][:]), ["wv0"], ["wv0"])
            tt("dve", wv[1][:], Aim[:], wv[0][:], ALU.mult, ["Aim", "wv0"], ["wv1"])
            cx.op("dve", lambda e: e.tensor_scalar(out=wv[1][:], in0=wv[1][:], scalar1=-1.0, scalar2=None, op0=ALU.mult),
                  ["wv1"], ["wv1"])
            tt("dve", wv[0][:], Are[:], wv[0][:], ALU.mult, ["Are", "wv0"], ["wv0"])
            cmul("dve", sq2[0][:], sq2[1][:], sq2[0][:], sq[1][:], wv[0][:], wv[1][:], [64, 32],
                 ["sq20", "sq1", "wv0", "wv1"], ["sq20x", "sq21x"]) if False else None
            wq = [sb1(f"b_wq{r}", [64, 32], F32) for r in range(2)]
            cmul("dve", wq[0][:], wq[1][:], sq2[0][:], sq[1][:], wv[0][:], wv[1][:], [64, 32],
                 ["sq20", "sq1", "wv0", "wv1"], ["wq0", "wq1"])
            cmul("dve", Bb[0][:], Bb[1][:], wq[0][:, :].unsqueeze(2).broadcast_to([64, 32, 16]),
                 wq[1][:, :].unsqueeze(2).broadcast_to([64, 32, 16]), Bm[0][:], Bm[1][:], [64, 32, 16],
                 ["wq0", "wq1", ("Bm", 0), ("Bm", 1)], [("Bb", 0), ("Bb", 1)])
            tt("dve", wv[0][:], sq[0][:], sq[0][:], ALU.mult, ["sq0"], ["wv0"])
            tt("dve", wv[1][:], sq[1][:], sq[1][:], ALU.mult, ["sq1"], ["wv1"])
            tt("dve", wv[0][:], wv[0][:], wv[1][:], ALU.add, ["wv0", "wv1"], ["wv0"])
            cx.op("dve", lambda e: e.reciprocal(out=wv[0][:], in_=wv[0][:]), ["wv0"], ["wv0"])
            tt("dve", wv[1][:], sq[1][:], wv[0][:], ALU.mult, ["sq1", "wv0"], ["wv1"])
            cx.op("dve", lambda e: e.tensor_scalar(out=wv[1][:], in0=wv[1][:], scalar1=-1.0, scalar2=None, op0=ALU.mult),
                  ["wv1"], ["wv1"])
            tt("dve", wv[0][:], sq[0][:], wv[0][:], ALU.mult, ["sq0", "wv0"], ["wv0"])

            def powers(P_, base, nmax, key):
                cx.op("dve", lambda e: e.memset(P_[0][:, :, 0:1], 1.0), (), [(key, 0)])
                cx.op("dve", lambda e: e.memset(P_[1][:, :, 0:1], 0.0), (), [(key, 1)])
                cur = [sq2[0], sq2[1]]
                nxt = [wq[0], wq[1]]
                cx.op("dve", lambda e: e.tensor_copy(out=cur[0][:], in_=base[0][:]), ["basek0" + key], ["cur0"])
                cx.op("dve", lambda e: e.tensor_copy(out=cur[1][:], in_=base[1][:]), ["basek1" + key], ["cur1"])
                k = 1
                while k < nmax:
                    hi = min(2 * k, nmax)
                    w_ = hi - k
                    cmul("dve", P_[0][:, :, k:hi], P_[1][:, :, k:hi], P_[0][:, :, 0:w_], P_[1][:, :, 0:w_],
                         cur[0][:, :].unsqueeze(2).broadcast_to([64, 32, w_]),
                         cur[1][:, :].unsqueeze(2).broadcast_to([64, 32, w_]), [64, 32, w_],
                         [(key, 0), (key, 1), "cur0", "cur1"], [(key, 0), (key, 1)])
                    cmul("dve", nxt[0][:], nxt[1][:], cur[0][:], cur[1][:], cur[0][:], cur[1][:], [64, 32],
                         ["cur0", "cur1"], ["nxt0", "nxt1"])
                    cx.op("dve", lambda e: e.tensor_copy(out=cur[0][:], in_=nxt[0][:]), ["nxt0"], ["cur0"])
                    cx.op("dve", lambda e: e.tensor_copy(out=cur[1][:], in_=nxt[1][:]), ["nxt1"], ["cur1"])
                    k *= 2
                return cur

            cx.last_w["basek0PW"] = cx.last_w.get("sq0")
            cx.last_w["basek1PW"] = cx.last_w.get("sq1")
            cur = powers(PW, sq, 32, "PW")
            for r in range(2):
                cx.op("dve", lambda e, r=r: e.tensor_copy(out=PW[r][:, :, 32:33], in_=cur[r][:, :].unsqueeze(2)),
                      [f"cur{r}"], [("PW", r)])
                cx.op("dve", lambda e, r=r: e.tensor_copy(out=lam32[r][:], in_=cur[r][:]), [f"cur{r}"], [("l32", r)])
            cx.last_w["basek0IPW"] = cx.last_w.get("wv0")
            cx.last_w["basek1IPW"] = cx.last_w.get("wv1")
            powers(IPW, wv, 8, "IPW")
            for r in range(2):
                cx.op("pool", lambda e, r=r: e.tensor_copy(out=RPW[r][:, 0:16, :], in_=PW[r][:, 0:16, 31::-1]),
                      [("PW", r)], [("RPW", r)])
                cx.op("pool", lambda e, r=r: e.tensor_copy(out=RPW[r][:, 16:32, :], in_=PW[r][:, 16:32, 32:0:-1]),
                      [("PW", r)], [("RPW", r)])

            if "dbgPW" in S:
                for r in range(2):
                    cx.dma("sp", S["dbgPW"][r], PW[r][:], [("PW", r)], [("dPW", r)])
                    cx.dma("sp", S["dbgRPW"][r], RPW[r][:], [("RPW", r)], [("dRPW", r)])
                    cx.dma("sp", S["dbgIPW"][r], IPW[r][:], [("IPW", r)], [("dIPW", r)])
                    cx.dma("sp", S["dbgBb"][r], Bb[r][:], [("Bb", r)], [("dBb", r)])
                    cx.dma("sp", S["dbgCm"][r], Cm[r][:], [("Cm", r)], [("dCm", r)])
            if bstop == "b1":
                cx.barrier()
                return

            def bc_j(T, z0, z1, j0, j1):
                return T[:, z0:z1, j0:j1].unsqueeze(3).broadcast_to([64, z1 - z0, j1 - j0, 16])

            def bc_h(T, z0, z1, nj):
                return T[:, z0:z1, :].unsqueeze(2).broadcast_to([64, z1 - z0, nj, 16])

            ev = 0
            eU = ExitStack()
            Uc = eU.enter_context(nc.sbuf_tensor("b_Uc", [128, CH, 256], BF16))
            Ug = eU.enter_context(nc.sbuf_tensor("b_Ug", [128, 16, CH, 16], BF16))
            cx.dma("sp", Uc[:], S["HT"][:, C_S5:C_S5 + 256].rearrange("(n s) c -> n s c", s=CH), (), ["Uc0"])
            for hf in range(2):
                cx.op("pool" if hf else "dve", lambda e, hf=hf: e.tensor_copy(
                    out=Ug[:, hf * 8:(hf + 1) * 8], in_=Uc[:, :, hf * 128:(hf + 1) * 128].rearrange("n s (g h) -> n g s h", h=16)),
                    ["Uc0"], [("Ug", hf)])
            for g in range(16):
                p = pqb[g % 2]
                pk = ("pqb", g % 2)
                for a in range(4):
                    cx.op("pe", lambda e, p=p, g=g, a=a: e.transpose(
                        p[:, a * 128:(a + 1) * 128], Ug[:, g, a * 8:(a + 1) * 8, :].rearrange("n s h -> n (s h)"), C["idb"][:]),
                        [("Ug", g // 8), "idb"], [pk], sig=(a == 3))
                evac(cx, ev, Ut[:, g, :, :], p[:, :].rearrange("p (a n) -> p a n", n=128), [pk], [("Ut", g)])
                ev += 1
            if bstop == "b2":
                cx.barrier()
                eU.close()
                return
            cx.barrier()
            eU.close()
            with ExitStack() as e2:
                sb2 = lambda name, shape, dt: e2.enter_context(nc.sbuf_tensor(name, shape, dt))
                RBf_ = [sb2(f"b_RBf{r}", [128, 4, 32, 16], BF16) for r in range(3)]
                PBb_ = [sb2(f"b_PBb{r}", [128, 4, 32, 16], BF16) for r in range(3)]
                IBf_ = [sb2(f"b_IBf{r}", [128, 4, 8, 16], BF16) for r in range(3)]
                ICb_ = [sb2(f"b_ICb{r}", [128, 4, 8, 16], BF16) for r in range(2)]
                PCq_ = [sb2(f"b_PCq{r}", [128, 4, 32, 16], BF16) for r in range(2)]
                for lst, nm in ((RBf_, "RBf"), (PBb_, "PBb"), (IBf_, "IBf"), (ICb_, "ICb"), (PCq_, "PCq")):
                    for r, T_ in enumerate(lst):
                        cx.op("pool", lambda e, T_=T_: e.memset(T_[64:128], 0.0), (), [(nm, r)])
                RBf = [T_[0:64] for T_ in RBf_]
                PBb = [T_[0:64] for T_ in PBb_]
                IBf = [T_[0:64] for T_ in IBf_]
                ICb = [T_[0:64] for T_ in ICb_]
                PCq = [T_[0:64] for T_ in PCq_]
                GinT = sb2("b_GinT", [128, 2, 4, 4, 2, 64], BF16)
                t0a = sb2("b_t0a", [128, 128], F32)
                t0b = sb2("b_t0b", [128, 128], F32)
                for q4 in range(4):
                    g0 = q4 * 4
                    rdk = [("PW", 0), ("PW", 1), ("RPW", 0), ("RPW", 1), ("IPW", 0), ("IPW", 1), ("Bb", 0), ("Bb", 1),
                           ("Cm", 0), ("Cm", 1)]
                    cmul("dve", RBf[0][:], RBf[1][:], bc_j(RPW[0], g0, g0 + 4, 0, 32), bc_j(RPW[1], g0, g0 + 4, 0, 32),
                         bc_h(Bb[0], g0, g0 + 4, 32), bc_h(Bb[1], g0, g0 + 4, 32), [64, 4, 32, 16], rdk,
                         [("RBf", 0), ("RBf", 1)])
                    cmul("dve", PBb[0][:], PBb[1][:], bc_j(PW[0], 16 + g0, 20 + g0, 0, 32), bc_j(PW[1], 16 + g0, 20 + g0, 0, 32),
                         bc_h(Bb[0], 16 + g0, 20 + g0, 32), bc_h(Bb[1], 16 + g0, 20 + g0, 32), [64, 4, 32, 16], rdk,
                         [("PBb", 0), ("PBb", 1)])
                    cmul("dve", IBf[0][:], IBf[1][:], bc_j(IPW[0], g0, g0 + 4, 0, 8), bc_j(IPW[1], g0, g0 + 4, 0, 8),
                         bc_h(Bb[0], g0, g0 + 4, 8), bc_h(Bb[1], g0, g0 + 4, 8), [64, 4, 8, 16], rdk,
                         [("IBf", 0), ("IBf", 1)])
                    cmul("dve", ICb[0][:], ICb[1][:], bc_j(IPW[0], 16 + g0, 20 + g0, 0, 8), bc_j(IPW[1], 16 + g0, 20 + g0, 0, 8),
                         bc_h(Cm[0], 16 + g0, 20 + g0, 8), bc_h(Cm[1], 16 + g0, 20 + g0, 8), [64, 4, 8, 16], rdk,
                         [("ICb", 0), ("ICb", 1)])
                    cmul("dve", PCq[0][:], PCq[1][:], bc_j(PW[0], g0, g0 + 4, 0, 32), bc_j(PW[1], g0, g0 + 4, 0, 32),
                         bc_h(Cm[0], g0, g0 + 4, 32), bc_h(Cm[1], g0, g0 + 4, 32), [64, 4, 32, 16], rdk,
                         [("PCq", 0), ("PCq", 1)])
                    for (T, nm) in ((PBb, "PBb"), (IBf, "IBf")):
                        cx.op("act", lambda e, T=T: e.activation(out=T[2][:], in_=T[1][:], func=AF.Copy, scale=-1.0),
                              [(nm, 1)], [(nm, 2)])
                    if bstop == "b2a":
                        cx.barrier()
                        return
                    for gi in range(4):
                        g = g0 + gi
                        if bstop == "b2b" and gi == 1:
                            cx.barrier()
                            return
                        pF, pFk = pq[0], ("pq", 0)
                        pB, pBk = pq[1], ("pq", 1)
                        if "tbuild" not in PIECES:
                            continue
                        mm(cx, pF[:, :], IBf_[0][:, gi, :, :].rearrange("p s h -> p (s h)"),
                           PCq_[0][:, gi, :, :].rearrange("p t h -> p (t h)"), True, False,
                           [("IBf", 0), ("PCq", 0)], [pFk], sig=False)
                        mm(cx, pF[:, :], IBf_[2][:, gi, :, :].rearrange("p s h -> p (s h)"),
                           PCq_[1][:, gi, :, :].rearrange("p t h -> p (t h)"), False, True,
                           [("IBf", 2), ("PCq", 1)], [pFk])
                        pB, pBk = pq[1], ("pq", 1)
                        for e_ in range(4):
                            mm(cx, pB[:, e_ * 128:(e_ + 1) * 128],
                               PBb_[0][:, gi, e_ * 8:(e_ + 1) * 8, :].rearrange("p s h -> p (s h)"),
                               ICb_[0][:, gi, :, :].rearrange("p t h -> p (t h)"), True, False,
                               [("PBb", 0), ("ICb", 0)], [pBk], sig=False)
                            mm(cx, pB[:, e_ * 128:(e_ + 1) * 128],
                               PBb_[2][:, gi, e_ * 8:(e_ + 1) * 8, :].rearrange("p s h -> p (s h)"),
                               ICb_[1][:, gi, :, :].rearrange("p t h -> p (t h)"), False, True,
                               [("PBb", 2), ("ICb", 1)], [pBk], sig=(e_ == 3))
                        if "tgasm" not in PIECES:
                            continue
                        for d_ in range(1, 4):
                            cx.op("act", lambda e, g=g, d_=d_: e.activation(out=Tg[:, g, 3 + d_, :],
                                                                            in_=pF[:, d_ * 128:(d_ + 1) * 128],
                                                                            func=AF.Copy), [pFk], [("Tg", g)])
                        for e_ in range(1, 4):
                            cx.op("act", lambda e, g=g, e_=e_: e.activation(out=Tg[:, g, 3 - e_, :],
                                                                            in_=pB[:, e_ * 128:(e_ + 1) * 128],
                                                                            func=AF.Copy), [pBk], [("Tg", g)])
                        cx.op("act", lambda e: e.activation(out=t0a[:], in_=pF[:, 0:128], func=AF.Copy), [pFk], ["t0a"])
                        cx.op("act", lambda e: e.activation(out=t0b[:], in_=pB[:, 0:128], func=AF.Copy), [pBk], ["t0b"])
                        tt("dve", t0a[:], t0a[:], C["s5_mF"][:], ALU.mult, ["t0a"], ["t0a"])
                        tt("dve", t0b[:], t0b[:], C["s5_mB"][:], ALU.mult, ["t0b"], ["t0b"])
                        tt("dve", t0a[:], t0a[:], t0b[:], ALU.add, ["t0a", "t0b"], ["t0a"])
                        cx.op("dve", lambda e, g=g: e.scalar_tensor_tensor(
                            out=t0b[:], in0=C["idf"][:], scalar=dcolT[:, g:g + 1], in1=t0a[:], op0=ALU.mult,
                            op1=ALU.add), ["t0a", "dcol", "idf"], ["t0b"])
                        cx.op("act", lambda e, g=g: e.activation(out=Tg[:, g, 3, :], in_=t0b[:], func=AF.Copy),
                              ["t0b"], [("Tg", g)])
                        if "gint" not in PIECES:
                            continue
                        for z, T in ((0, RBf_), (1, PBb_)):
                            for r in range(2):
                                pb_, pbk = pqb[r], ("pqb", r)
                                for a in range(4):
                                    cx.op("pe", lambda e, T=T, a=a, r=r, pb_=pb_: e.transpose(
                                        pb_[:, a * 128:(a + 1) * 128],
                                        T[r][:, gi, a * 8:(a + 1) * 8, :].rearrange("p s h -> p (s h)"), C["idb"][:]),
                                        [("RBf" if z == 0 else "PBb", r), "idb"], [pbk], sig=(a == 3))
                                evac(cx, ev, GinT[:, z, gi, :, r, :],
                                     pb_[:, :].rearrange("p (a q) -> p a q", q=128)[:, :, 0:64], [pbk], [("GinT", z, gi)])
                                ev += 1
                        if "gproj" not in PIECES:
                            continue
                        for z in range(2):
                            for r in range(2):
                                pG, pGk = pg[r], ("pg", r)
                                for a in range(4):
                                    mm(cx, pG[:, :], GinT[:, z, gi, a, r, :], Ut[:, g, a, :], a == 0, a == 3,
                                       [("GinT", z, gi), ("Ut", g)], [pGk])
                                evac(cx, ev, Hc[r][:, z * 16 + g, :], pG[:, :], [pGk], [("H", r, z * 16 + g)])
                                ev += 1
                cx.barrier()
            cx.barrier()
        cx.barrier()
        if "dbgTg" in S:
            cx.dma("sp", S["dbgTg"], Tg[:], [("Tg", g) for g in range(16)], ["dTg"])
            for r in range(2):
                cx.dma("sp", S["dbgG"][r], Hc[r][:], [("H", r, k) for k in range(32)], [("dG", r)])
        if "dbgTg" in S:
            cx.barrier()
        if bstop == "b3":
            cx.barrier()
            return
        Hsh = [[sb(f"b_Hsh{z}{r}", [64, 16, NCK], BF16) for r in range(2)] for z in range(2)]
        with ExitStack() as e3:
            sb3 = lambda name, shape, dt: e3.enter_context(nc.sbuf_tensor(name, shape, dt))
            A1 = [sb3(f"b_A1{z}", [64, 2, 16, 2], F32) for z in range(2)]
            A2 = [sb3(f"b_A2{z}", [64, 2, 16, 2], F32) for z in range(2)]
            pv = [sb3(f"b_pv{z}", [64, 2, 16, 2], F32) for z in range(2)]
            ps_ = [sb3(f"b_ps{z}", [64, 2, 16, 2], F32) for z in range(2)]
            w1 = [sb3(f"b_w1{z}", [64, 2, 16, 2], F32) for z in range(2)]
            w2 = [sb3(f"b_w2{z}", [64, 2, 16, 2], F32) for z in range(2)]
            allH = [("H", r, k) for r in range(2) for k in range(32)]
            for z in range(2):
                eng = "dve" if z == 0 else "pool"
                zs = slice(16 * z, 16 * z + 16)
                for sq_i in range(2):
                    cx.op(eng, lambda e, z=z, sq_i=sq_i: e.tensor_copy(out=A1[z][:, 0, :, sq_i], in_=lam32[0][:, zs]),
                          [("l32", 0)], [("A1", z)])
                    cx.op(eng, lambda e, z=z, sq_i=sq_i: e.tensor_copy(out=A1[z][:, 1, :, sq_i], in_=lam32[0][:, zs]),
                          [("l32", 0)], [("A1", z)])
                    cx.op(eng, lambda e, z=z, sq_i=sq_i: e.tensor_copy(out=A2[z][:, 1, :, sq_i], in_=lam32[1][:, zs]),
                          [("l32", 1)], [("A2", z)])
                    cx.op(eng, lambda e, z=z, sq_i=sq_i: e.tensor_scalar(out=A2[z][:, 0, :, sq_i], in0=lam32[1][:, zs],
                                                                         scalar1=-1.0, scalar2=None, op0=ALU.mult),
                          [("l32", 1)], [("A2", z)])

            def hview(r, z, n):
                return Hc[r][:, 16 * z:16 * z + 16, :].rearrange("p g (q n) -> p g q n", q=2)[:, :, :, n]

            first = True
            for i in range(1, NPS):
                for z in range(2):
                    eng = "dve" if z == 0 else "pool"
                    n = i if z == 0 else NPS - 1 - i
                    npv = n - 1 if z == 0 else n + 1
                    rd = allH if i == 1 else [("Hs", z)]
                    for r in range(2):
                        cx.op(eng, lambda e, r=r, z=z, npv=npv: e.tensor_copy(out=pv[z][:, r], in_=hview(r, z, npv)),
                              rd, [("pv", z)])
                        cx.op(eng, lambda e, r=r, z=z, npv=npv: e.tensor_copy(out=ps_[z][:, 1 - r], in_=hview(r, z, npv)),
                              rd, [("ps", z)])
                    cx.op(eng, lambda e, z=z: e.tensor_tensor(out=w1[z][:], in0=A1[z][:], in1=pv[z][:], op=ALU.mult),
                          [("A1", z), ("pv", z)], [("w1", z)])
                    cx.op(eng, lambda e, z=z: e.tensor_tensor(out=w2[z][:], in0=A2[z][:], in1=ps_[z][:], op=ALU.mult),
                          [("A2", z), ("ps", z)], [("w2", z)])
                    cx.op(eng, lambda e, z=z: e.tensor_tensor(out=w1[z][:], in0=w1[z][:], in1=w2[z][:], op=ALU.add),
                          [("w1", z), ("w2", z)], [("w1", z)])
                    for r in range(2):
                        cx.op(eng, lambda e, r=r, z=z, n=n: e.tensor_tensor(out=hview(r, z, n), in0=hview(r, z, n),
                                                                           in1=w1[z][:, r], op=ALU.add),
                              [("w1", z)] + (allH if i == 1 else []), [("Hs", z)])
            for z in range(2):
                for r in range(2):
                    src = Hc[r][:, 16 * z:16 * z + 16, :].rearrange("p g (q n) -> p g q n", q=2)
                    dst = Hsh[z][r][:, :, :].rearrange("p g (q n) -> p g q n", q=2)
                    zc = 0 if z == 0 else NPS - 1
                    cx.op("pool", lambda e, dst=dst, zc=zc: e.memset(dst[:, :, :, zc:zc + 1], 0.0), (), [("Hsh", z, r)])
                    sgn = 1.0 if r == 0 else -1.0
                    if z == 0:
                        o_, i_ = dst[:, :, :, 1:NPS], src[:, :, :, 0:NPS - 1]
                    else:
                        o_, i_ = dst[:, :, :, 0:NPS - 1], src[:, :, :, 1:NPS]
                    cx.op("act", lambda e, o_=o_, i_=i_, sgn=sgn: e.activation(out=o_, in_=i_, func=AF.Copy, scale=sgn),
                          [("Hs", z)] + allH, [("Hsh", z, r)])
        cx.barrier()
        if "dbgH" in S:
            for r in range(2):
                cx.dma("sp", S["dbgH"][r], Hc[r][:], [("Hs", 0), ("Hs", 1)] + [("H", r, k) for k in range(32)], [("dH", r)])
        if bstop == "b4":
            cx.barrier()
            return
        with ExitStack() as e4:
            sb4 = lambda name, shape, dt: e4.enter_context(nc.sbuf_tensor(name, shape, dt))
            Zc = sb4("b_Zc", [128, CH, 256], BF16)
            zT = sb4("b_zT", [128, 2, NTOK], BF16)
            wg = sb4("b_wg", [128, 2, 512], BF16)
            bcol = sb4("b_bcol", [128, 4], F32)
            ga = [sb4(f"b_ga{i}", [128, 512], F32) for i in range(2)]
            gb = [sb4(f"b_gb{i}", [128, 512], F32) for i in range(2)]
            sg = [sb4(f"b_sg{i}", [128, 512], F32) for i in range(2)]
            ya = [sb4(f"b_ya{i}", [128, 2, 512], BF16) for i in range(2)]
            pY = [e4.enter_context(nc.psum_tensor(f"b_pY{i}", [128, 512], F32)) for i in range(2)]
            pZ = [e4.enter_context(nc.psum_tensor(f"b_pZ{i}", [128, 512], BF16)) for i in range(2)]
            pV = [e4.enter_context(nc.psum_tensor(f"b_pV{i}", [128, 512], F32)) for i in range(4)]
            cx.dma("pool", wg[:], prm["w_glu"].rearrange("(k p) c -> p k c", p=128), (), ["wg"])
            cx.dma("sp", bcol[:], prm["b_glu"].rearrange("(m p) -> p m", p=128), (), ["bcol"],
                   allow_slow_non_contiguous=True)
            t1 = sb4("b_t1o", [64, 2048], F32)
            t2 = sb4("b_t2o", [64, 2048], F32)
            CoF = [sb4(f"b_CoF{r}", [64, 4, 32, 16], BF16) for r in range(2)]
            CoB = [sb4(f"b_CoB{r}", [64, 4, 32, 16], BF16) for r in range(2)]

            def tt4(eng, out, a, b, op, rd, wr):
                return cx.op(eng, lambda e: e.tensor_tensor(out=out, in0=a, in1=b, op=op), rd, wr)

            def cmul4(eng, outr, outi, ar, ai, br, bi, rd, wr):
                v1 = t1[:, :].rearrange("p (a b c) -> p a b c", b=32, c=16)
                v2 = t2[:, :].rearrange("p (a b c) -> p a b c", b=32, c=16)
                tt4(eng, v1, ar, br, ALU.mult, rd, ["t1o"])
                tt4(eng, v2, ai, bi, ALU.mult, rd, ["t2o"])
                tt4(eng, outr, v1, v2, ALU.subtract, ["t1o", "t2o"], wr[0:1])
                tt4(eng, v1, ar, bi, ALU.mult, rd, ["t1o"])
                tt4(eng, v2, ai, br, ALU.mult, rd, ["t2o"])
                tt4(eng, outi, v1, v2, ALU.add, ["t1o", "t2o"], wr[1:2])

            def bcj(T, z0, z1, j0, j1):
                return T[:, z0:z1, j0:j1].unsqueeze(3).broadcast_to([64, z1 - z0, j1 - j0, 16])

            def bch(T, z0, z1, nj):
                return T[:, z0:z1, :].unsqueeze(2).broadcast_to([64, z1 - z0, nj, 16])

            for g in range(16):
                if g % 4 == 0:
                    g0 = g
                    rdk = [("PW", 0), ("PW", 1), ("RPW", 0), ("RPW", 1), ("Cm", 0), ("Cm", 1)]
                    cmul4("dve", CoF[0][:], CoF[1][:], bcj(PW[0], g0, g0 + 4, 1, 33), bcj(PW[1], g0, g0 + 4, 1, 33),
                          bch(Cm[0], g0, g0 + 4, 32), bch(Cm[1], g0, g0 + 4, 32), rdk, [("CoF", 0), ("CoF", 1)])
                    cmul4("dve", CoB[0][:], CoB[1][:], bcj(RPW[0], 16 + g0, 20 + g0, 0, 32), bcj(RPW[1], 16 + g0, 20 + g0, 0, 32),
                          bch(Cm[0], 16 + g0, 20 + g0, 32), bch(Cm[1], 16 + g0, 20 + g0, 32), rdk, [("CoB", 0), ("CoB", 1)])
                i2 = g % 2
                pY_, pYk = pY[i2], ("pY", i2)
                for a in range(4):
                    mm(cx, pY_[:, :], Ut[:, g, a, :], Tg[:, g, 3 - a:7 - a, :].rearrange("p d c -> p (d c)"),
                       a == 0, False, [("Ut", g), ("Tg", g)], [pYk], sig=False)
                k = 0
                for z, Co in ((0, CoF), (1, CoB)):
                    for r in range(2):
                        rhs = Co[r][:, g % 4, :, :].rearrange("p t h -> p (t h)")
                        mm(cx, pY_[:, :], Hsh[z][r][:, g, :], rhs, False, k == 3,
                           [("Hsh", z, r), ("CoF", r), ("CoB", r)], [pYk])
                        k += 1
                cx.op("act", lambda e: e.activation(out=ga[i2][:], in_=pY_[:, :], func=AF.Square), [pYk], [("ga", i2)])
                cx.op("dve", lambda e: e.tensor_scalar(out=ga[i2][:], in0=ga[i2][:], scalar1=0.044715, scalar2=1.0,
                                                      op0=ALU.mult, op1=ALU.add), [("ga", i2)], [("ga", i2)])
                cx.op("dve", lambda e: e.tensor_tensor(out=gb[i2][:], in0=pY_[:, :], in1=ga[i2][:], op=ALU.mult),
                      [pYk, ("ga", i2)], [("gb", i2)])
                cx.op("act", lambda e: e.activation(out=sg[i2][:], in_=gb[i2][:], func=AF.Sigmoid, scale=1.5957691216),
                      [("gb", i2)], [("sg", i2)])
                cx.op("dve", lambda e, g=g: e.tensor_tensor(out=Zc[:, :, g * 16:(g + 1) * 16],
                                                           in0=pY_[:, :].rearrange("p (t h) -> p t h", h=16),
                                                           in1=sg[i2][:, :].rearrange("p (t h) -> p t h", h=16),
                                                           op=ALU.mult), [pYk, ("sg", i2)], [("Zc", g)])
            ev = 0
            allZ = [("Zc", g) for g in range(16)]
            for half in range(2):
                for t4 in range(CH // 4):
                    p, pk = pZ[ev % 2], ("pZ", ev % 2)
                    for tt_ in range(4):
                        t = t4 * 4 + tt_
                        cx.op("pe", lambda e, p=p, t=t, tt_=tt_: e.transpose(p[:, tt_ * 128:(tt_ + 1) * 128],
                                                                             Zc[:, t, half * 128:(half + 1) * 128],
                                                                             C["idb"][:]), allZ + ["idb"], [pk],
                              sig=(tt_ == 3))
                    out = zT[:, half, :].rearrange("p (n t) -> p n t", t=CH)[:, :, t4 * 4:(t4 + 1) * 4]
                    evac(cx, ev, out, p[:, :].rearrange("p (t n) -> p n t", n=128), [pk], [("zT", half, t4)])
                    ev += 1
            allzT = [("zT", h_, t4) for h_ in range(2) for t4 in range(CH // 4)]
            for tb in range(NTOK // 512):
                i2 = tb % 2
                tsl = slice(tb * 512, (tb + 1) * 512)
                for m in range(4):
                    pv_, pvk = pV[m], ("pV", m)
                    for k in range(2):
                        mm(cx, pv_[:, :], wg[:, k, m * 128:(m + 1) * 128], zT[:, k, tsl], k == 0, k == 1,
                           ["wg"] + allzT, [pvk])
                for j in range(2):
                    cx.op("act", lambda e, j=j: e.activation(out=sg[i2][:], in_=pV[2 + j][:, :], func=AF.Sigmoid,
                                                            bias=bcol[:, 2 + j:3 + j]), [("pV", 2 + j), "bcol"],
                          [("sg", i2)])
                    cx.op("dve", lambda e, j=j: e.scalar_tensor_tensor(out=ya[i2][:, j, :], in0=pV[j][:, :],
                                                                      scalar=bcol[:, j:j + 1], in1=sg[i2][:],
                                                                      op0=ALU.add, op1=ALU.mult),
                          [("pV", j), ("sg", i2), "bcol"], [("ya", i2, j)])
                cx.dma("sp", S["YT"][0:256, tsl].rearrange("(j p) t -> p j t", p=128), ya[i2][:],
                       [("ya", i2, 0), ("ya", i2, 1)], [("YTa", tb)])
        cx.barrier()


def ln_tile(cx, pT, pTk, xres, xres_k, gB, bB, T, m_key, out_dram):
    xr, st, mv, rs, nm, xn = T["xr"], T["st"], T["mv"], T["rs"], T["nm"], T["xn"]
    cx.op("dve", lambda e: e.scalar_tensor_tensor(out=xr[:], in0=xres, scalar=ALPHA, in1=pT, op0=ALU.mult, op1=ALU.add),
          [xres_k] + pTk, ["xr"])
    for i in range(2):
        cx.op("dve", lambda e, i=i: e.bn_stats(out=st[:, i, :], in_=xr[:, i * 512:(i + 1) * 512]), ["xr"], ["st"])
    cx.op("dve", lambda e: e.bn_aggr(out=mv[:], in_=st[:, :, :].rearrange("p a b -> p (a b)")), ["st"], ["mv"])
    cx.op("act", lambda e: e.activation(out=rs[:], in_=mv[:, 1:2], func=AF.Ln, bias=LN_EPS), ["mv"], ["rs"])
    cx.op("act", lambda e: e.activation(out=rs[:], in_=rs[:], func=AF.Exp, scale=-0.5), ["rs"], ["rs"])
    cx.op("dve", lambda e: e.scalar_tensor_tensor(out=nm[:], in0=mv[:, 0:1], scalar=-1.0, in1=rs[:], op0=ALU.mult,
                                                 op1=ALU.mult), ["mv", "rs"], ["nm"])
    cx.op("act", lambda e: e.activation(out=xn[:], in_=xr[:], func=AF.Identity, scale=rs[:, 0:1], bias=nm[:, 0:1]),
          ["xr", "rs", "nm"], ["xn"])
    cx.op("pool", lambda e: e.tensor_tensor(out=xn[:], in0=xn[:], in1=gB[:], op=ALU.mult), ["xn", "gB"], ["xn"])
    cx.op("dve", lambda e: e.tensor_tensor(out=xn[:], in0=xn[:], in1=bB[:], op=ALU.add), ["xn", "bB"], ["xn"])
    cx.dma("sp", out_dram, xn[:], ["xn"], [m_key])


def ln_bufs(nc, es, pfx):
    sb = lambda name, shape, dt: es.enter_context(nc.sbuf_tensor(pfx + name, shape, dt))
    return dict(xr=sb("xr", [128, D], F32), st=sb("st", [128, 2, 6], F32), mv=sb("mv", [128, 2], F32),
                rs=sb("rs", [128, 1], F32), nm=sb("nm", [128, 1], F32), xn=sb("xn", [128, D], F32))


def phase_E(cx, C, S, X, w_out, ln_g, ln_b, X1):
    nc = cx.nc
    with ExitStack() as es:
        sb = lambda name, shape, dt: es.enter_context(nc.sbuf_tensor(name, shape, dt))
        wo = sb("e_wo", [128, 8, D], BF16)
        gB, bB = sb("e_gB", [128, D], F32), sb("e_bB", [128, D], F32)
        yt = [sb(f"e_yt{i}", [128, 8, 512], BF16) for i in range(2)]
        xin = [sb(f"e_xin{i}", [128, 4, D], F32) for i in range(2)]
        mixT = sb("e_mixT", [128, 8, 512], F32)
        T = ln_bufs(nc, es, "e_")
        pM = [es.enter_context(nc.psum_tensor(f"e_pM{i}", [128, 512], F32)) for i in range(3)]
        pT = [es.enter_context(nc.psum_tensor(f"e_pT{i}", [128, D], F32)) for i in range(2)]
        for k in range(8):
            cx.dma("pool", wo[:, k, :], w_out[k * 128:(k + 1) * 128, :], (), [("wo", k)])
        cx.dma("sp", gB[:], ln_g.partition_broadcast(128), (), ["gB"])
        cx.dma("sp", bB[:], ln_b.partition_broadcast(128), (), ["bB"])
        ev = 0
        pmi = 0
        pti = 0
        for tb in range(NTOK // 512):
            s_ = tb % 2
            tsl = slice(tb * 512, (tb + 1) * 512)
            cx.dma("sp", yt[s_][:], S["YT"][:, tsl].rearrange("(k p) t -> p k t", p=128), (), [("yt", s_)])
            cx.dma("sp", xin[s_][:], X[tsl, :].rearrange("(m p) d -> p m d", p=128), (), [("xin", s_)])
            for c in range(8):
                ps, pk = pM[pmi % 3], ("pM", pmi % 3)
                pmi += 1
                for k in range(8):
                    mm(cx, ps[:, :], wo[:, k, c * 128:(c + 1) * 128], yt[s_][:, k, :], k == 0, k == 7,
                       [("wo", k), ("yt", s_)], [pk])
                evac(cx, ev, mixT[:, c, :], ps[:, :], [pk], [("mixT", c)])
                ev += 1
            for m in range(4):
                p_, pk = pT[pti % 2], ("pT", pti % 2)
                pti += 1
                for c in range(8):
                    cx.op("pe", lambda e, p_=p_, c=c, m=m: e.transpose(p_[:, c * 128:(c + 1) * 128],
                                                                      mixT[:, c, m * 128:(m + 1) * 128], C["idf"][:]),
                          [("mixT", c), "idf"], [pk], sig=(c == 7))
                r0 = tb * 512 + m * 128
                ln_tile(cx, p_[:, :], [pk], xin[s_][:, m, :], ("xin", s_), gB, bB, T, ("X1", tb, m), X1[r0:r0 + 128, :])
        cx.barrier()


def phase_F(cx, C, X1, w1, w2, ln_g, ln_b, X2):
    nc = cx.nc
    TB = 512
    with ExitStack() as es:
        sb = lambda name, shape, dt: es.enter_context(nc.sbuf_tensor(name, shape, dt))
        gB, bB = sb("f_gB", [128, D], F32), sb("f_bB", [128, D], F32)
        xin = sb("f_xin", [128, 4, D], F32)
        xb = sb("f_xb", [128, 4, D], BF16)
        xT = sb("f_xT", [128, 8, TB], BF16)
        hidT = sb("f_hidT", [128, 32, TB], BF16)
        outT = sb("f_outT", [128, 8, TB], F32)
        w1g = [sb(f"f_w1g{i}", [128, 8, 512], BF16) for i in range(3)]
        w2g = [sb(f"f_w2g{i}", [128, 32, 256], BF16) for i in range(2)]
        rl = [sb(f"f_rl{i}", [128, TB], F32) for i in range(2)]
        T = ln_bufs(nc, es, "f_")
        pTb = [es.enter_context(nc.psum_tensor(f"f_pTb{i}", [128, 512], BF16)) for i in range(2)]
        pM = [es.enter_context(nc.psum_tensor(f"f_pM{i}", [128, 512], F32)) for i in range(3)]
        pT = [es.enter_context(nc.psum_tensor(f"f_pT{i}", [128, D], F32)) for i in range(1)]
        cx.dma("sp", gB[:], ln_g.partition_broadcast(128), (), ["gB"])
        cx.dma("sp", bB[:], ln_b.partition_broadcast(128), (), ["bB"])
        ev = 0
        pmi = 0
        g1 = 0
        g2 = 0
        for tb in range(NTOK // TB):
            tsl = slice(tb * TB, (tb + 1) * TB)
            cx.dma("sp", xin[:], X1[tsl, :].rearrange("(m p) d -> p m d", p=128), (), ["xin"])
            cx.dma("pool", xb[:], X1[tsl, :].rearrange("(m p) d -> p m d", p=128), (), ["xb"])
            for k in range(8):
                pt, ptk = pTb[k % 2], ("pTb", k % 2)
                for m in range(4):
                    cx.op("pe", lambda e, pt=pt, m=m, k=k: e.transpose(pt[:, m * 128:(m + 1) * 128],
                                                                      xb[:, m, k * 128:(k + 1) * 128], C["idb"][:]),
                          ["xb", "idb"], [ptk], sig=(m == 3))
                evac(cx, ev, xT[:, k, :], pt[:, :], [ptk], [("xT", k)])
                ev += 1
            allxT = [("xT", k) for k in range(8)]
            for jg in range(8):
                sl = g1 % 3
                g1 += 1
                cx.dma("pool", w1g[sl][:], w1[:, jg * 512:(jg + 1) * 512].rearrange("(k p) c -> p k c", p=128),
                       (), [("w1g", sl)])
                for jj in range(4):
                    j = jg * 4 + jj
                    ps, pk = pM[pmi % 3], ("pM", pmi % 3)
                    pmi += 1
                    for k in range(8):
                        mm(cx, ps[:, :], w1g[sl][:, k, jj * 128:(jj + 1) * 128], xT[:, k, :], k == 0, k == 7,
                           [("w1g", sl)] + allxT, [pk])
                    r_ = rl[j % 2]
                    if j % 2 == 0:
                        cx.op("act", lambda e, r_=r_, ps=ps: e.activation(out=r_[:], in_=ps[:, :], func=AF.Relu),
                              [pk], [("rl", j % 2)])
                        cx.op("dve", lambda e, r_=r_, j=j: e.tensor_tensor(out=hidT[:, j, :], in0=r_[:], in1=r_[:],
                                                                          op=ALU.mult), [("rl", j % 2)], [("hid", j)])
                    else:
                        cx.op("dve", lambda e, r_=r_, ps=ps: e.tensor_scalar(out=r_[:], in0=ps[:, :], scalar1=0.0,
                                                                            scalar2=None, op0=ALU.max),
                              [pk], [("rl", j % 2)])
                        cx.op("act", lambda e, r_=r_, j=j: e.activation(out=hidT[:, j, :], in_=r_[:], func=AF.Square),
                              [("rl", j % 2)], [("hid", j)])
            allhid = [("hid", j) for j in range(32)]
            for cg in range(4):
                sl = g2 % 2
                g2 += 1
                cx.dma("pool", w2g[sl][:], w2[:, cg * 256:(cg + 1) * 256].rearrange("(j p) c -> p j c", p=128),
                       (), [("w2g", sl)])
                for cc in range(2):
                    c = cg * 2 + cc
                    ps, pk = pM[pmi % 3], ("pM", pmi % 3)
                    pmi += 1
                    for j in range(32):
                        mm(cx, ps[:, :], w2g[sl][:, j, cc * 128:(cc + 1) * 128], hidT[:, j, :], j == 0, j == 31,
                           [("w2g", sl)] + allhid, [pk])
                    evac(cx, ev, outT[:, c, :], ps[:, :], [pk], [("outT", c)])
                    ev += 1
            for m in range(4):
                p_, pk = pT[0], ("pT", 0)
                for c in range(8):
                    cx.op("pe", lambda e, p_=p_, c=c, m=m: e.transpose(p_[:, c * 128:(c + 1) * 128],
                                                                      outT[:, c, m * 128:(m + 1) * 128], C["idf"][:]),
                          [("outT", c), "idf"], [pk], sig=(c == 7))
                r0 = tb * TB + m * 128
                ln_tile(cx, p_[:, :], [pk], xin[:, m, :], "xin", gB, bB, T, ("X2", tb, m), X2[r0:r0 + 128, :])
        cx.barrier()

S5_NAMES = (("a_re", [DEPTH, 2, 16, 64]), ("a_im", [DEPTH, 2, 16, 64]), ("log_step", [DEPTH, 2, 16, 64]),
            ("b_re", [DEPTH, 2, 16, 64, 16]), ("b_im", [DEPTH, 2, 16, 64, 16]),
            ("c_re", [DEPTH, 2, 16, 16, 64]), ("c_im", [DEPTH, 2, 16, 16, 64]), ("d", [DEPTH, 16, 16]),
            ("w_glu", [DEPTH, 256, 512]), ("b_glu", [DEPTH, 512]))


def build(stop_after="all", debug=False):
    nc = bass.Bass("TRN2", target_bir_lowering=False)
    okind = "ExternalOutput" if debug else "Internal"
    inp = lambda name, shape: nc.dram_tensor(name, shape, F32, kind="ExternalInput").ap()
    x = inp("x", [NTOK, D])
    w_in = inp("w_in", [DEPTH, D, D_IN])
    swa_sink = inp("swa_sink", [DEPTH, 8])
    gla_w_a = inp("gla_w_a", [DEPTH, 2, 16, 128])
    gla_b_a = inp("gla_b_a", [DEPTH, 2, 128])
    gla_ln_g = inp("gla_ln_g", [DEPTH, 256])
    s5p = {nm: inp("s5_" + nm, shp) for nm, shp in S5_NAMES}
    w_out = inp("w_out", [DEPTH, D, D])
    ln1_g, ln1_b = inp("ln1_g", [DEPTH, D]), inp("ln1_b", [DEPTH, D])
    w_ff1, w_ff2 = inp("w_ff1", [DEPTH, D, D_FF]), inp("w_ff2", [DEPTH, D_FF, D])
    ln2_g, ln2_b = inp("ln2_g", [DEPTH, D]), inp("ln2_b", [DEPTH, D])
    S = {}
    S["HT"] = nc.dram_tensor("HT", [NTOK, TM_COLS], BF16, kind=okind).ap()
    S["HF"] = nc.dram_tensor("HF", [4, 128, NTOK], BF16, kind=okind).ap()
    S["HLR"] = nc.dram_tensor("HLR", [32, NTOK], F32, kind=okind).ap()
    S["YT"] = nc.dram_tensor("YT", [D, NTOK], BF16, kind=okind).ap()
    X1 = nc.dram_tensor("X1", [NTOK, D], F32, kind=okind).ap()
    X2 = nc.dram_tensor("X2", [NTOK, D], F32, kind=okind).ap()
    y = nc.dram_tensor("y", [NTOK, D], F32, kind="ExternalOutput").ap()
    ncp = NCP(nc)
    cx = Cx(ncp)
    with ExitStack() as es:
        C = make_consts(cx, es)
        make_swa_consts(cx, es, C)
        make_gla_consts(cx, es, C)
        make_s5_consts(cx, es, C)
        if debug and stop_after.lower().startswith("b"):
            for nm, shp in (("dbgPW", [2, 64, 32, 33]), ("dbgRPW", [2, 64, 32, 32]), ("dbgIPW", [2, 64, 32, 8]),
                            ("dbgBb", [2, 64, 32, 16]), ("dbgCm", [2, 64, 32, 16]), ("dbgG", [2, 64, 32, 128]),
                            ("dbgH", [2, 64, 32, 128])):
                S[nm] = nc.dram_tensor(nm, shp, F32, kind="ExternalOutput").ap()
            S["dbgTg"] = nc.dram_tensor("dbgTg", [128, 16, 7, 128], BF16, kind="ExternalOutput").ap()
        for l in range(DEPTH):
            ncp.suffix = f"_L{l}"
            Xl = x if l == 0 else X2
            Xo = X2 if l < DEPTH - 1 else y
            phase_A(cx, C, Xl, w_in[l], S)
            if stop_after == "A":
                return nc
            if stop_after.lower().startswith("b"):
                phase_B(cx, C, S, {k: v[l] for k, v in s5p.items()},
                        bstop=(stop_after if stop_after.startswith("b") else None))
                return nc
            if stop_after == "C":
                phase_C(cx, C, S, gla_w_a[l], gla_b_a[l], gla_ln_g[l])
                return nc
            if stop_after == "D":
                phase_D(cx, C, S, swa_sink[l])
                return nc
            phase_B(cx, C, S, {k: v[l] for k, v in s5p.items()})
            phase_C(cx, C, S, gla_w_a[l], gla_b_a[l], gla_ln_g[l])
            phase_D(cx, C, S, swa_sink[l])
            if stop_after == "Y":
                return nc
            phase_E(cx, C, S, Xl, w_out[l], ln1_g[l], ln1_b[l], X1)
            if stop_after == "E":
                return nc
            phase_F(cx, C, X1, w_ff1[l], w_ff2[l], ln2_g[l], ln2_b[l], Xo)
            if stop_after == "F":
                return nc
    return nc


IN_NAMES = ("w_in", "s5_a_re", "s5_a_im", "s5_log_step", "s5_b_re", "s5_b_im", "s5_c_re", "s5_c_im", "s5_d",
            "s5_w_glu", "s5_b_glu", "gla_w_a", "gla_b_a", "gla_ln_g", "swa_sink", "w_out", "ln1_g", "ln1_b",
            "w_ff1", "w_ff2", "ln2_g", "ln2_b")


def make_in_maps(inputs, ncores=NCORES):
    maps = []
    for c in range(ncores):
        m = {k: np.ascontiguousarray(inputs[k], dtype=np.float32) for k in IN_NAMES}
        m["x"] = np.ascontiguousarray(inputs["x"][c * NSEQ:(c + 1) * NSEQ], dtype=np.float32).reshape(NTOK, D)
        maps.append(m)
    return maps


def kernel(**inputs):
    nc = build()
    res = run_bass_kernel_spmd(nc, make_in_maps(inputs), core_ids=list(range(NCORES)))
    out = np.concatenate([np.asarray(r["y"], dtype=np.float32).reshape(NSEQ, L, D) for r in res.results], axis=0)
    return out
```
